# Optimizing a Trainium2 kernel written in Bass

```python
import jax
import jax.numpy as jnp
from jax import lax
import numpy as np

D_MODEL = 2048
BATCH = 2
SEQ = 4096
DEPTH = 2

N_META = 16
CHUNK = 64
PAD = CHUNK - N_META
D_FF = 5504
FFN_RES = 0.5
EPS = 1e-6
N_BRANCH = 3
BRANCH_WIDTH = 1024

GLA_HEADS = 4
GLA_DK = 128
GLA_DV = 256
GLA_GATE_RANK = 16
GLA_GATE_TAU = 16.0

RET_HEADS = 4
RET_DK = 256
RET_DV = 256
ROPE_BASE = 10000.0

HGRN_HEADS = 8
HGRN_DK = 128
HGRN_DV = 128
FORGET_FLOOR = 1e-20

IN_SPLITS = (
    GLA_HEADS * GLA_DK, GLA_HEADS * GLA_DK, GLA_HEADS * GLA_DV, GLA_HEADS * GLA_DV, GLA_GATE_RANK,
    RET_HEADS * RET_DK, RET_HEADS * RET_DK, RET_HEADS * RET_DV, RET_HEADS * RET_DV,
    HGRN_HEADS * HGRN_DK, HGRN_HEADS * HGRN_DK, HGRN_HEADS * HGRN_DV, HGRN_HEADS * HGRN_DV,
    N_BRANCH * D_MODEL,
)
W_IN = sum(IN_SPLITS)

kernel_name = 'hybrid_gla_retnet_hgrn2_macaron_block'


def rms_norm(x, g):
    xf = x.astype(jnp.float32)
    y = xf * lax.rsqrt(jnp.mean(xf * xf, axis=-1, keepdims=True) + EPS)
    return (y * g.astype(jnp.float32)).astype(x.dtype)


def head_rms_norm(y, n_heads, g):
    b, l, w = y.shape
    yf = y.astype(jnp.float32).reshape(b, l, n_heads, w // n_heads)
    yf = yf * lax.rsqrt(jnp.mean(yf * yf, axis=-1, keepdims=True) + EPS)
    return yf.reshape(b, l, w) * g.astype(jnp.float32)


def head_group_norm(y, n_heads, g):
    b, l, w = y.shape
    yf = y.astype(jnp.float32).reshape(b, l, n_heads, w // n_heads)
    yf = yf - jnp.mean(yf, axis=-1, keepdims=True)
    yf = yf * lax.rsqrt(jnp.mean(yf * yf, axis=-1, keepdims=True) + EPS)
    return yf.reshape(b, l, w) * g.astype(jnp.float32)


def swiglu(h, w_gate, w_up, w_down):
    return (jax.nn.silu(h @ w_gate) * (h @ w_up)) @ w_down


def rotary(t, pos):
    half = t.shape[-1] // 2
    inv_freq = ROPE_BASE ** (-jnp.arange(half, dtype=jnp.float32) / half)
    ang = pos[:, None] * inv_freq[None, :]
    cos = jnp.cos(ang)[None, :, None, :]
    sin = jnp.sin(ang)[None, :, None, :]
    t1, t2 = t[..., :half], t[..., half:]
    return jnp.concatenate([t1 * cos - t2 * sin, t1 * sin + t2 * cos], axis=-1)


def to_chunks(t, n_heads):
    b, l, w = t.shape
    t = jnp.pad(t.astype(jnp.float32), ((0, 0), (PAD, 0), (0, 0)))
    t = t.reshape(b, (l + PAD) // CHUNK, CHUNK, n_heads, w // n_heads)
    return t.transpose(1, 0, 3, 2, 4)


def from_chunks(o):
    nc, b, h, c, dv = o.shape
    return o.transpose(1, 0, 3, 2, 4).reshape(b, nc * c, h * dv)[:, PAD:]


def chunk_gated_linear_attention(q, k, v, log_a):
    causal = jnp.tril(jnp.ones((CHUNK, CHUNK), dtype=bool))[:, :, None]
    nc, b, h, c, dk = q.shape
    dv = v.shape[-1]

    def step(state, inp):
        qc, kc, vc, gc = inp
        cum = jnp.cumsum(gc, axis=-2)
        diff = cum[..., :, None, :] - cum[..., None, :, :]
        decay = jnp.where(causal, jnp.exp(jnp.where(causal, diff, 0.0)), 0.0)
        scores = jnp.einsum('bhid,bhjd,bhijd->bhij', qc, kc, decay)
        o = (jnp.einsum('bhij,bhjv->bhiv', scores, vc)
             + jnp.einsum('bhid,bhdv->bhiv', qc * jnp.exp(cum), state))
        last = cum[..., -1:, :]
        state = (jnp.exp(last[..., 0, :])[..., None] * state
                 + jnp.einsum('bhjd,bhjv->bhdv', kc * jnp.exp(last - cum), vc))
        return state, o

    s0 = jnp.zeros((b, h, dk, dv), jnp.float32)
    _, o = lax.scan(step, s0, (q, k, v, log_a))
    return o


def chunk_retention(q, k, v, log_gamma):
    idx = jnp.arange(CHUNK, dtype=jnp.float32)
    rel = idx[:, None] - idx[None, :]
    causal = (rel >= 0)[None]
    intra = jnp.where(causal, jnp.exp(jnp.where(causal, rel[None], 0.0) * log_gamma[:, None, None]), 0.0)
    q_decay = jnp.exp((idx[None, :] + 1.0) * log_gamma[:, None])[..., None]
    k_decay = jnp.exp((CHUNK - 1.0 - idx[None, :]) * log_gamma[:, None])[..., None]
    chunk_decay = jnp.exp(CHUNK * log_gamma)[:, None, None]
    nc, b, h, c, dk = q.shape
    dv = v.shape[-1]

    def step(state, inp):
        qc, kc, vc = inp
        scores = jnp.einsum('bhid,bhjd->bhij', qc, kc) * intra
        o = (jnp.einsum('bhij,bhjv->bhiv', scores, vc)
             + jnp.einsum('bhid,bhdv->bhiv', qc * q_decay, state))
        state = chunk_decay * state + jnp.einsum('bhjd,bhjv->bhdv', kc * k_decay, vc)
        return state, o

    s0 = jnp.zeros((b, h, dk, dv), jnp.float32)
    _, o = lax.scan(step, s0, (q, k, v))
    return o


def hybrid_mixer(u, w_in, gla_w_gate2, gla_b_gate, gla_norm, ret_norm, hgrn_lb, hgrn_norm, w_branch, w_out):
    b, l, _ = u.shape
    f32 = jnp.float32
    split_at = [int(s) for s in np.cumsum(IN_SPLITS)[:-1]]
    (gq, gk, gv, gg, g_lr, rq, rk, rv, rg, hq, hf, hi, hg, mg) = jnp.split(u @ w_in, split_at, axis=-1)

    g_log = jax.nn.log_sigmoid((g_lr @ gla_w_gate2 + gla_b_gate).astype(f32)) / GLA_GATE_TAU
    o = chunk_gated_linear_attention(
        to_chunks(gq.astype(f32) * GLA_DK ** -0.5, GLA_HEADS), to_chunks(gk, GLA_HEADS),
        to_chunks(gv, GLA_HEADS), to_chunks(g_log, GLA_HEADS))
    y_gla = head_rms_norm(from_chunks(o), GLA_HEADS, gla_norm) * jax.nn.silu(gg.astype(f32))

    pos = jnp.arange(l, dtype=f32)
    rq = rotary(rq.astype(f32).reshape(b, l, RET_HEADS, RET_DK), pos).reshape(b, l, -1) * RET_DK ** -0.5
    rk = rotary(rk.astype(f32).reshape(b, l, RET_HEADS, RET_DK), pos).reshape(b, l, -1)
    log_gamma = jnp.log(1.0 - 2.0 ** (-5.0 - jnp.arange(RET_HEADS, dtype=f32)))
    o = chunk_retention(to_chunks(rq, RET_HEADS), to_chunks(rk, RET_HEADS), to_chunks(rv, RET_HEADS), log_gamma)
    y_ret = head_group_norm(from_chunks(o), RET_HEADS, ret_norm) * jax.nn.silu(rg.astype(f32))

    hf = hf.astype(f32)
    lb = hgrn_lb.astype(f32)
    forget = lb + (1.0 - lb) * jax.nn.sigmoid(hf)
    log_f = jnp.log(jnp.maximum(forget, FORGET_FLOOR))
    k_in = (1.0 - lb) * jax.nn.sigmoid(-hf)
    i_in = jax.nn.silu(hi.astype(f32))
    o = chunk_gated_linear_attention(
        to_chunks(hq, HGRN_HEADS), to_chunks(k_in, HGRN_HEADS),
        to_chunks(i_in, HGRN_HEADS), to_chunks(log_f, HGRN_HEADS))
    y_hgrn = head_rms_norm(from_chunks(o), HGRN_HEADS, hgrn_norm) * jax.nn.silu(hg.astype(f32))

    ys = jnp.stack([y_gla, y_ret, y_hgrn], axis=2).astype(u.dtype)
    gates = jax.nn.sigmoid(mg.astype(f32)).reshape(b, l, N_BRANCH, D_MODEL).astype(u.dtype)
    merged = (jnp.einsum('blnw,nwd->blnd', ys, w_branch) * gates).sum(axis=2)
    return merged @ w_out


def setup_inputs(seed: int = 0) -> dict:
    key = jax.random.key(seed)
    ks = jax.random.split(key, 24)
    f32 = jnp.float32

    def nrm(k, shape, scale):
        return jax.random.normal(k, shape, f32) * scale

    def gain(k, shape):
        return 1.0 + 0.02 * jax.random.normal(k, shape, f32)

    return {
        'x': nrm(ks[0], (BATCH, SEQ, D_MODEL), 1.0),
        'meta_tokens': nrm(ks[1], (N_META, D_MODEL), 1.0),
        'ffn1_norm': gain(ks[2], (DEPTH, D_MODEL)),
        'ffn1_w_gate': nrm(ks[3], (DEPTH, D_MODEL, D_FF), D_MODEL ** -0.5),
        'ffn1_w_up': nrm(ks[4], (DEPTH, D_MODEL, D_FF), D_MODEL ** -0.5),
        'ffn1_w_down': nrm(ks[5], (DEPTH, D_FF, D_MODEL), D_FF ** -0.5),
        'mix_norm': gain(ks[6], (DEPTH, D_MODEL)),
        'w_in': nrm(ks[7], (DEPTH, D_MODEL, W_IN), D_MODEL ** -0.5),
        'gla_w_gate2': nrm(ks[8], (DEPTH, GLA_GATE_RANK, GLA_HEADS * GLA_DK), GLA_GATE_RANK ** -0.5),
        'gla_b_gate': nrm(ks[9], (DEPTH, GLA_HEADS * GLA_DK), 0.1),
        'gla_norm': gain(ks[10], (DEPTH, GLA_HEADS * GLA_DV)),
        'ret_norm': gain(ks[11], (DEPTH, RET_HEADS * RET_DV)),
        'hgrn_lb_logits': nrm(ks[12], (DEPTH, HGRN_HEADS * HGRN_DK), 0.1),
        'hgrn_norm': gain(ks[13], (DEPTH, HGRN_HEADS * HGRN_DV)),
        'w_branch': nrm(ks[14], (DEPTH, N_BRANCH, BRANCH_WIDTH, D_MODEL), BRANCH_WIDTH ** -0.5),
        'w_out': nrm(ks[15], (DEPTH, D_MODEL, D_MODEL), D_MODEL ** -0.5),
        'ffn2_norm': gain(ks[16], (DEPTH, D_MODEL)),
        'ffn2_w_gate': nrm(ks[17], (DEPTH, D_MODEL, D_FF), D_MODEL ** -0.5),
        'ffn2_w_up': nrm(ks[18], (DEPTH, D_MODEL, D_FF), D_MODEL ** -0.5),
        'ffn2_w_down': nrm(ks[19], (DEPTH, D_FF, D_MODEL), D_FF ** -0.5),
        'final_norm': gain(ks[20], (D_MODEL,)),
    }


def reference(x, meta_tokens, ffn1_norm, ffn1_w_gate, ffn1_w_up, ffn1_w_down, mix_norm, w_in,
              gla_w_gate2, gla_b_gate, gla_norm, ret_norm, hgrn_lb_logits, hgrn_norm, w_branch, w_out,
              ffn2_norm, ffn2_w_gate, ffn2_w_up, ffn2_w_down, final_norm):
    b = x.shape[0]
    meta = jnp.broadcast_to(meta_tokens[None].astype(x.dtype), (b, N_META, D_MODEL))
    h = jnp.concatenate([meta, x], axis=1)
    lb_soft = jax.nn.softmax(hgrn_lb_logits.astype(jnp.float32), axis=0)
    lower_bounds = jnp.cumsum(lb_soft, axis=0) - lb_soft[0]
    for layer in range(DEPTH):
        h = h + FFN_RES * swiglu(rms_norm(h, ffn1_norm[layer]),
                                 ffn1_w_gate[layer], ffn1_w_up[layer], ffn1_w_down[layer])
        h = h + hybrid_mixer(rms_norm(h, mix_norm[layer]), w_in[layer], gla_w_gate2[layer], gla_b_gate[layer],
                             gla_norm[layer], ret_norm[layer], lower_bounds[layer], hgrn_norm[layer],
                             w_branch[layer], w_out[layer])
        h = h + FFN_RES * swiglu(rms_norm(h, ffn2_norm[layer]),
                                 ffn2_w_gate[layer], ffn2_w_up[layer], ffn2_w_down[layer])
    return rms_norm(h, final_norm)[:, N_META:]
```

```python
import numpy as np
import concourse.bass as bass
import concourse.mybir as mybir
from concourse.bass_utils import run_bass_kernel_spmd

F32 = mybir.dt.float32
BF16 = mybir.dt.bfloat16
AF = mybir.ActivationFunctionType
ALU = mybir.AluOpType

D = 2048
NDC = 16
T = 1040
NMETA = 16
DFF = 5504
NFC = 43
EPS = 1e-6
TT = [(16, 512), (528, 512), (0, 16)]
FGROUPS = [(0, 11), (11, 11), (22, 11), (33, 10)]
GF = 11


class Op:
    __slots__ = ("eng", "fn", "deps", "is_dma", "semkey", "needs_signal", "sig", "dma_ord", "gi", "inc")


class Sched:
    ENGS = ("pe", "act", "dve", "pool", "sp")

    def __init__(self):
        self.ops = []
        self.last_writer = {}
        self.readers = {}
        self.dma_count = {}
        self.out_dmas = []
        self.extra = []
        self.last_eng = {}
        self.last_dma = {}

    def add(self, eng, fn, reads=(), writes=(), dma=None, is_out=False, inc=16):
        op = Op()
        op.inc = inc
        op.eng = eng
        op.fn = fn
        op.is_dma = dma is not None
        op.semkey = dma
        op.needs_signal = False
        op.sig = None
        op.gi = len(self.ops)
        deps = set()
        for k in reads:
            w = self.last_writer.get(k)
            if w is not None:
                deps.add(w)
        for k in writes:
            w = self.last_writer.get(k)
            if w is not None:
                deps.add(w)
            for r in self.readers.get(k, ()):
                deps.add(r)
        deps.update(self.extra)
        deps.discard(op)
        op.deps = deps
        if dma is None:
            self.last_eng[eng] = op
        else:
            self.last_dma[dma] = op
        for k in writes:
            self.last_writer[k] = op
            self.readers[k] = []
        for k in reads:
            self.readers.setdefault(k, []).append(op)
        if op.is_dma:
            n = self.dma_count.get(dma, 0) + 1
            self.dma_count[dma] = n
            op.dma_ord = n
            if is_out:
                self.out_dmas.append(op)
        self.ops.append(op)
        return op

    def barrier(self):
        self.extra = list(self.last_eng.values()) + list(self.last_dma.values())

    def emit(self, nc, block_engines, sems):
        per_eng = {e: [] for e in self.ENGS}
        for op in self.ops:
            per_eng[op.eng].append(op)
        for op in self.ops:
            best = {}
            for d in op.deps:
                if d.is_dma:
                    k = ("dma", d.semkey)
                else:
                    if d.eng == "pe" and op.eng == "pe" and not op.is_dma:
                        continue
                    k = ("eng", d.eng)
                o = best.get(k)
                if o is None or d.gi > o.gi:
                    best[k] = d
            op.deps = list(best.values())
        for op in self.ops:
            for d in op.deps:
                if not d.is_dma:
                    d.needs_signal = True
        for e in self.ENGS:
            n = 0
            for op in per_eng[e]:
                if op.is_dma:
                    continue
                if op.needs_signal:
                    n += 1
                    op.sig = n
        self.sig_totals = {}
        for e in self.ENGS:
            self.sig_totals[e] = sum(1 for op in per_eng[e] if (not op.is_dma and op.needs_signal))

        def run_engine(ename, eng):
            waited = {}
            for op in per_eng[ename]:
                need = {}
                for d in op.deps:
                    if d.is_dma:
                        key = ("dma", d.semkey)
                        val = d.inc * d.dma_ord
                    else:
                        if d.eng == "pe" and ename == "pe" and not op.is_dma:
                            continue
                        key = ("eng", d.eng)
                        val = d.sig
                    if val > need.get(key, 0):
                        need[key] = val
                for key, val in need.items():
                    if waited.get(key, 0) >= val:
                        continue
                    eng.wait_ge(sems[key], val)
                    waited[key] = val
                ins = op.fn(eng)
                if op.is_dma:
                    ins.then_inc(sems[("dma", op.semkey)], op.inc)
                elif op.needs_signal:
                    ins.then_inc(sems[("eng", ename)], 1)
            if ename == "sp":
                final = {}
                for op in self.out_dmas:
                    key = ("dma", op.semkey)
                    final[key] = max(final.get(key, 0), 16 * self.dma_count[op.semkey])
                for key, val in final.items():
                    eng.wait_ge(sems[key], val)

        for ename, reg in block_engines.items():
            def mk(ename=ename):
                def f(eng):
                    run_engine(ename, eng)
                return f
            reg(mk())


GQ, GK, GV, GG, GLR, RQ, RK, RV, RG, HQ, HF, HI, HGC, MG = (
    0, 512, 1024, 2048, 3072, 3088, 4112, 5136, 6160, 7184, 8208, 9232, 10256, 11280)
BR = {
    "gla": dict(nh=4, KC=1, VC=2, q=GQ, k=GK, v=GV, g=GG, qs=128 ** -0.5, ys0=0, ncol=64),
    "ret": dict(nh=4, KC=2, VC=2, q=RQ, k=RK, v=RV, g=RG, qs=256 ** -0.5, ys0=8, ncol=72),
    "hg": dict(nh=8, KC=1, VC=1, q=HQ, k=HF, v=HI, g=HGC, qs=1.0, ys0=16, ncol=80),
}
TM = [(0, 16)] + [(16 + 128 * i, 128) for i in range(8)]
ARENA_W = 26000
P_FFN1, P_MIX, P_FFN2, P_FIN, P_LB0, P_LB1 = 0, 16, 32, 48, 88, 96
RET_LOGG = [float(np.log(1.0 - 2.0 ** (-5.0 - h))) for h in range(4)]


def ukey(tok0):
    return 0 if tok0 < 16 else (16 if tok0 < 528 else 528)


class Ctx:
    pass


def build_program(stages):
    nc = bass.Bass("TRN2", target_bir_lowering=False)
    S = Sched()
    C = Ctx()
    C.nc, C.S = nc, S
    C.dram = {}
    C.blocks = []
    C.blk_off = 0

    def din(name, shape, dt=F32):
        C.dram[name] = nc.dram_tensor(name, list(shape), dt, kind="ExternalInput").ap()
        return C.dram[name]

    def dout(name, shape, dt=F32):
        C.dram[name] = nc.dram_tensor(name, list(shape), dt, kind="ExternalOutput").ap()
        return C.dram[name]

    def dint(name, shape, dt=F32):
        C.dram[name] = nc.dram_tensor(name, list(shape), dt, kind="Internal").ap()
        return C.dram[name]

    C.din, C.dout, C.dint = din, dout, dint
    from contextlib import ExitStack
    with ExitStack() as es:
        def sb(name, shape, dt):
            return es.enter_context(nc.sbuf_tensor("sb_" + name, list(shape), dt))
        C.sb = sb
        C.hT = sb("hT", [128, NDC, T], F32)
        C.uT = sb("uT", [128, NDC, T], BF16)
        C.rstd = sb("rstd", [128, T], F32)
        C.ones = sb("ones", [128, 128], BF16)
        C.onesf = sb("onesf", [128, 128], F32)
        C.prm = sb("prm", [128, 512], F32)
        C.cm = sb("cm", [128, 8], F32)
        C.pb = 0
        C.L = 0
        C.epsb = sb("epsb", [128, 2], F32)
        C.cU = sb("cU", [128, 128], F32)
        C.cMq = sb("cMq", [128, 132], F32)
        C.cMq16 = sb("cMq16", [128, 20], F32)
        C.cmask = sb("cmask", [128, 128], F32)
        C.arena = sb("arena", [128, ARENA_W], F32)
        C.psum = es.enter_context(nc.psum_tensor("psum_all", [128, 8, 512], F32))
        C.bank_i = 0
        C.rr = {}

        def nextbank():
            b = C.bank_i % 8
            C.bank_i += 1
            return b
        C.nextbank = nextbank

        def rot(name, n):
            i = C.rr.get(name, 0)
            C.rr[name] = i + 1
            return i % n
        C.rot = rot

        class Arena:
            def __init__(self):
                self.off = 0

            def f32(self, shape):
                n = int(np.prod(shape))
                v = C.arena[:, self.off:self.off + n]
                self.off += n
                assert self.off <= ARENA_W, self.off
                if len(shape) == 2:
                    return v.rearrange("p (a b) -> p a b", a=shape[0])
                if len(shape) == 3:
                    return v.rearrange("p (a b c) -> p a b c", a=shape[0], b=shape[1])
                return v

            def bf16(self, shape):
                n = int(np.prod(shape))
                w = (n + 1) // 2
                v = C.arena[:, self.off:self.off + w].bitcast(BF16)
                self.off += w
                assert self.off <= ARENA_W, self.off
                if len(shape) == 2:
                    return v.rearrange("p (a b) -> p a b", a=shape[0])
                if len(shape) == 3:
                    return v.rearrange("p (a b c) -> p a b c", a=shape[0], b=shape[1])
                return v
        C.Arena = Arena

        S.add("dve", lambda e: e.memset(C.ones[:], 1.0), writes=[("ones",)])
        S.add("dve", lambda e: e.memset(C.onesf[:], 1.0), writes=[("onesf",)])
        S.add("dve", lambda e: e.memset(C.epsb[:], EPS), writes=[("epsb",)])
        cst = din("cst", [128, 408])
        S.add("sp", lambda e: e.dma_start(out=C.cU[:], in_=cst[:, 0:128]), writes=[("cU",)], dma=("c", 0))
        S.add("sp", lambda e: e.dma_start(out=C.cMq[:], in_=cst[:, 128:260]), writes=[("cMq",)], dma=("c", 1))
        S.add("sp", lambda e: e.dma_start(out=C.cMq16[:], in_=cst[:, 260:280]), writes=[("cMq16",)], dma=("c", 2))
        S.add("sp", lambda e: e.dma_start(out=C.cmask[:], in_=cst[:, 280:408]), writes=[("cmask",)], dma=("c", 3))
        prm = din("prm", [128, 512])
        S.add("sp", lambda e: e.dma_start(out=C.prm[:], in_=prm), writes=[("prm",)], dma=("c", 4))
        cmd = din("cm", [128, 8])
        S.add("sp", lambda e: e.dma_start(out=C.cm[:], in_=cmd), writes=[("cm",)], dma=("c", 5))

        for st in stages:
            st[0](C, *st[1:])

        if C.blk_off > 0:
            din("wblk", [128, C.blk_off])
        sems = {}
        for e in Sched.ENGS:
            sems[("eng", e)] = es.enter_context(nc.semaphore(f"s_{e}"))
        for i, k in enumerate(S.dma_count.keys()):
            sems[("dma", k)] = es.enter_context(nc.semaphore(f"d_{i}"))
        C.n_sems = len(sems)
        with nc.Block() as block:
            S.emit(nc, {"pe": block.tensor, "act": block.scalar, "dve": block.vector,
                        "pool": block.gpsimd, "sp": block.sync}, sems)
    return nc, C


def st_load_h(C, name):
    x = C.din(name, [D, T])
    xv = x.rearrange("(c p) t -> p c t", p=128)
    for q in range(4):
        cs = slice(4 * q, 4 * q + 4)
        C.S.add("sp", lambda e, cs=cs: e.dma_start(out=C.hT[:, cs, :], in_=xv[:, cs, :]),
                writes=[("hT", c) for c in range(4 * q, 4 * q + 4)], dma=("ldh", q))


def st_store_h(C, name):
    o = C.dout(name, [D, T])
    ov = o.rearrange("(c p) t -> p c t", p=128)
    for q in range(4):
        cs = slice(4 * q, 4 * q + 4)
        C.S.add("sp", lambda e, cs=cs: e.dma_start(out=ov[:, cs, :], in_=C.hT[:, cs, :]),
                reads=[("hT", c) for c in range(4 * q, 4 * q + 4)], dma=("sth", q), is_out=True)


def rstd_from_ps(C, b, n, out_ap, okey, scale, part=128):
    S = C.S
    ps = C.psum
    S.add("act", lambda e: e.activation(out=out_ap, in_=ps[:part, b, :n], func=AF.Ln,
                                        bias=C.epsb[:part, 0:1], scale=scale),
          reads=[("ps", b), ("epsb",)], writes=[okey])
    S.add("act", lambda e: e.activation(out=out_ap, in_=out_ap, func=AF.Exp, scale=-0.5),
          reads=[okey], writes=[okey])


def rmsnorm(C, sqbufs, gcol0, inplace=False):
    S = C.S
    ps = C.psum
    for (t0, n) in TT:
        b = C.nextbank()
        for c in range(NDC):
            si = C.rot("sq", 3)
            S.add("act", lambda e, si=si, c=c, t0=t0, n=n: e.activation(
                out=sqbufs[si][:, :n], in_=C.hT[:, c, t0:t0 + n], func=AF.Square),
                reads=[("hT", c)], writes=[("sq", si)])
            S.add("pe", lambda e, si=si, c=c, b=b, n=n: e.matmul(
                ps[:, b, :n], C.ones[:], sqbufs[si][:, :n], start=(c == 0), stop=(c == NDC - 1)),
                reads=[("sq", si), ("ones",)], writes=[("ps", b)])
        rstd_from_ps(C, b, n, C.rstd[:, t0:t0 + n], ("rstd", t0), 1.0 / D)
    for c in range(NDC):
        for (t0, n) in TT:
            if not inplace:
                S.add("dve", lambda e, c=c, t0=t0, n=n: e.scalar_tensor_tensor(
                    out=C.uT[:, c, t0:t0 + n], in0=C.hT[:, c, t0:t0 + n],
                    scalar=C.prm[:, gcol0 + c:gcol0 + c + 1], in1=C.rstd[:, t0:t0 + n],
                    op0=ALU.mult, op1=ALU.mult),
                    reads=[("hT", c), ("prm",), ("rstd", t0)], writes=[("uT", c, t0)])
            else:
                S.add("dve", lambda e, c=c, t0=t0, n=n: e.scalar_tensor_tensor(
                    out=C.hT[:, c, t0:t0 + n], in0=C.hT[:, c, t0:t0 + n],
                    scalar=C.prm[:, gcol0 + c:gcol0 + c + 1], in1=C.rstd[:, t0:t0 + n],
                    op0=ALU.mult, op1=ALU.mult),
                    reads=[("hT", c), ("prm",), ("rstd", t0)], writes=[("hT", c)])


def st_final_norm(C):
    C.S.barrier()
    A = C.Arena()
    sq = [A.bf16([512]) for _ in range(3)]
    rmsnorm(C, sq, P_FIN, inplace=True)
    C.S.barrier()


def st_ffn(C, wgu_name, wd_name, gcol0):
    S = C.S
    ps = C.psum
    S.barrier()
    A = C.Arena()
    aT = A.bf16([GF, T])
    wgu = [A.bf16([2, NDC, 128]) for _ in range(3)]
    wdb = [A.bf16([GF, 256]) for _ in range(3)]
    sg = [A.bf16([512]) for _ in range(3)]
    sq = [A.bf16([512]) for _ in range(3)]
    if wgu_name not in C.dram:
        C.din(wgu_name, [NFC, 128, 2 * NDC * 128])
        C.din(wd_name, [4, 8, 128, GF * 256])
    wgu_d = C.dram[wgu_name]
    wd_d = C.dram[wd_name]
    rmsnorm(C, sq, C.pb + gcol0)
    for gi, (f0, nf) in enumerate(FGROUPS):
        for fi in range(nf):
            f = f0 + fi
            ws = C.rot("wgu", 3)
            S.add("pool", lambda e, ws=ws, f=f: e.dma_start(
                out=wgu[ws].rearrange("p a c m -> p (a c m)"), in_=wgu_d[f]),
                writes=[("wgu", ws)], dma=("wgu", ws))
            for (t0, n) in TT:
                bg = C.nextbank()
                for c in range(NDC):
                    S.add("pe", lambda e, ws=ws, c=c, bg=bg, t0=t0, n=n: e.matmul(
                        ps[:, bg, :n], wgu[ws][:, 0, c, :], C.uT[:, c, t0:t0 + n],
                        start=(c == 0), stop=(c == NDC - 1)),
                        reads=[("wgu", ws), ("uT", c, t0)], writes=[("ps", bg)])
                bu = C.nextbank()
                for c in range(NDC):
                    S.add("pe", lambda e, ws=ws, c=c, bu=bu, t0=t0, n=n: e.matmul(
                        ps[:, bu, :n], wgu[ws][:, 1, c, :], C.uT[:, c, t0:t0 + n],
                        start=(c == 0), stop=(c == NDC - 1)),
                        reads=[("wgu", ws), ("uT", c, t0)], writes=[("ps", bu)])
                si = C.rot("sg", 3)
                S.add("act", lambda e, si=si, bg=bg, n=n: e.activation(
                    out=sg[si][:, :n], in_=ps[:, bg, :n], func=AF.Silu),
                    reads=[("ps", bg)], writes=[("sg", si)])
                S.add("dve", lambda e, si=si, bu=bu, fi=fi, t0=t0, n=n: e.tensor_tensor(
                    out=aT[:, fi, t0:t0 + n], in0=sg[si][:, :n], in1=ps[:, bu, :n], op=ALU.mult),
                    reads=[("sg", si), ("ps", bu)], writes=[("aT", fi, t0)])
        for dp in range(8):
            ws = C.rot("wd", 3)
            S.add("pool", lambda e, ws=ws, gi=gi, dp=dp: e.dma_start(
                out=wdb[ws].rearrange("p f m -> p (f m)"), in_=wd_d[gi, dp]),
                writes=[("wd", ws)], dma=("wd", ws))
            for dd in range(2):
                dc = 2 * dp + dd
                for (t0, n) in TT:
                    b = C.nextbank()
                    for fi in range(nf):
                        S.add("pe", lambda e, ws=ws, fi=fi, dd=dd, b=b, t0=t0, n=n: e.matmul(
                            ps[:, b, :n], wdb[ws][:, fi, dd * 128:(dd + 1) * 128], aT[:, fi, t0:t0 + n],
                            start=(fi == 0), stop=(fi == nf - 1)),
                            reads=[("wd", ws), ("aT", fi, t0)], writes=[("ps", b)])
                    S.add("dve", lambda e, b=b, dc=dc, t0=t0, n=n: e.scalar_tensor_tensor(
                        out=C.hT[:, dc, t0:t0 + n], in0=ps[:, b, :n], scalar=0.5,
                        in1=C.hT[:, dc, t0:t0 + n], op0=ALU.mult, op1=ALU.add),
                        reads=[("ps", b), ("hT", dc)], writes=[("hT", dc)])
    S.barrier()


def load_block(C, wsl, src, RC, cols, cap=4096):
    cols = np.asarray(cols)
    n = RC * len(cols)
    assert n <= cap
    off = C.blk_off
    C.blk_off += n
    C.blocks.append(((src, C.L), RC, cols))
    ws = C.rot("wsl", 3)
    view = wsl[ws][:, 0:n].rearrange("p (c m) -> p c m", c=RC)
    C.S.add("pool", lambda e: e.dma_start(out=wsl[ws][:, 0:n], in_=C.dram["wblk"][:, off:off + n]),
            writes=[("wsl", ws)], dma=("wsl", ws))
    return ws, view


def proj_fm(C, ws, view, c0, m, t0, n):
    b = C.nextbank()
    uk = ukey(t0)
    for c in range(NDC):
        C.S.add("pe", lambda e, c=c: e.matmul(
            C.psum[0:m, b, 0:n], view[:, c, c0:c0 + m], C.uT[:, c, t0:t0 + n],
            start=(c == 0), stop=(c == NDC - 1)),
            reads=[("wsl", ws), ("uT", c, uk)], writes=[("ps", b)])
    return b


def proj_tm(C, ws, view, ncols, t0, n):
    b = C.nextbank()
    uk = ukey(t0)
    for c in range(NDC):
        C.S.add("pe", lambda e, c=c: e.matmul(
            C.psum[0:n, b, 0:ncols], C.uT[:, c, t0:t0 + n], view[:, c, 0:ncols],
            start=(c == 0), stop=(c == NDC - 1)),
            reads=[("wsl", ws), ("uT", c, uk)], writes=[("ps", b)])
    return b


def st_set_layer(C, L):
    C.L = L
    C.pb = 256 * L


def st_mixer(C, mode):
    S = C.S
    ps = C.psum
    full = (mode == "full")
    L = C.L
    S.barrier()
    A = C.Arena()
    sq = [A.bf16([512]) for _ in range(3)]
    if not full:
        rmsnorm(C, sq, C.pb + P_MIX)
    wsl = [A.bf16([4096]) for _ in range(3)]
    tbl = A.f32([4400])
    C.glr1T = A.f32([T])
    C.w2b = A.f32([512])
    B = Ctx()
    B.ktm, B.gt, B.e1, B.e2, B.eE = [A.f32([256]) for _ in range(5)]
    B.khat = [A.bf16([256]) for _ in range(2)]
    B.vt = [A.bf16([256]) for _ in range(2)]
    B.decs = A.f32([2, 4])
    B.Sst = A.f32([2, 256])
    B.dtot = A.f32([2])
    B.lbc = A.f32([40])
    B.xst = [A.f32([516]) for _ in range(2)]
    B.dpr = A.f32([2])
    if full:
        B.qraw, B.kraw, B.sgT = A.bf16([2, T]), A.bf16([2, T]), A.bf16([2, T])
        B.eq, B.ek = A.f32([2, 128]), A.f32([2, 128])
        B.qt = [A.bf16([2, 128]) for _ in range(2)]
        B.kt = [A.bf16([2, 128]) for _ in range(2)]
        B.Pm = [A.bf16([128]) for _ in range(2)]
        B.osb = A.f32([2, 128])
        B.sqo = A.bf16([2, 128])
        B.rs = A.f32([128])
        B.yst = [A.bf16([2, 128]) for _ in range(3)]
        B.Sb = A.bf16([4, 256])
        B.Sin = A.f32([2, 256])
        B.Sld = A.f32([2, 256])
        B.dsl = A.f32([2])
        B.fA, B.fB, B.fC = A.f32([512]), A.f32([512]), A.f32([512])
        if "ysd" not in C.dram:
            C.dint("ysd", [24, 128, T], BF16)
    w2b_d = C.din(f"w2b{L}", [17, 512]) if f"w2b{L}" not in C.dram else C.dram[f"w2b{L}"]
    S.add("sp", lambda e: e.dma_start(out=C.w2b[0:17, :], in_=w2b_d), writes=[("w2b",)], dma=("w2b",))
    S.add("dve", lambda e: e.memset(C.glr1T[:], 1.0), writes=[("glr",)])
    ws, view = load_block(C, wsl, "w_in", NDC, np.arange(GLR, GLR + 16))
    for (t0, n) in TT:
        b = proj_fm(C, ws, view, 0, 16, t0, n)
        S.add("act", lambda e, b=b, t0=t0, n=n: e.activation(
            out=C.glr1T[0:16, t0:t0 + n], in_=ps[0:16, b, 0:n], func=AF.Copy),
            reads=[("ps", b)], writes=[("glr",)])
    lbc = B.lbc
    s0c, s1c, lbcol, omlc, nomlc = (lbc[:, 0:8], lbc[:, 8:16], lbc[:, 16:24], lbc[:, 24:32], lbc[:, 32:40])
    l0c, l1c = C.prm[:, C.pb + P_LB0:C.pb + P_LB0 + 8], C.prm[:, C.pb + P_LB1:C.pb + P_LB1 + 8]
    flagc = C.prm[:, C.pb + 104:C.pb + 105]

    def lb_compute(l0, l1, s0, s1, lb, oml, noml, flag_ap, rk, wk):
        S.add("dve", lambda e: e.tensor_tensor(out=s1, in0=l0, in1=l1, op=ALU.subtract), reads=rk + wk, writes=wk)
        S.add("dve", lambda e: e.tensor_scalar(out=s0, in0=s1, scalar1=-1.0, scalar2=None, op0=ALU.mult), reads=wk, writes=wk)
        S.add("act", lambda e: e.activation(out=s0, in_=s0, func=AF.Exp), reads=wk, writes=wk)
        S.add("act", lambda e: e.activation(out=s1, in_=s1, func=AF.Exp), reads=wk, writes=wk)
        S.add("dve", lambda e: e.tensor_scalar(out=s0, in0=s0, scalar1=1.0, scalar2=None, op0=ALU.add), reads=wk, writes=wk)
        S.add("dve", lambda e: e.tensor_scalar(out=s1, in0=s1, scalar1=1.0, scalar2=None, op0=ALU.add), reads=wk, writes=wk)
        S.add("dve", lambda e: e.reciprocal(out=s0, in_=s0), reads=wk, writes=wk)
        S.add("dve", lambda e: e.reciprocal(out=s1, in_=s1), reads=wk, writes=wk)
        S.add("dve", lambda e: e.scalar_tensor_tensor(out=lb, in0=s1, scalar=flag_ap, in1=s0, op0=ALU.mult, op1=ALU.add),
              reads=wk + rk, writes=wk)
        S.add("dve", lambda e: e.tensor_tensor(out=lb, in0=lb, in1=s0, op=ALU.subtract), reads=wk, writes=wk)
        S.add("dve", lambda e: e.tensor_scalar(out=oml, in0=lb, scalar1=-1.0, scalar2=1.0, op0=ALU.mult, op1=ALU.add),
              reads=wk, writes=wk)
        if noml is not None:
            S.add("dve", lambda e: e.tensor_scalar(out=noml, in0=oml, scalar1=-1.0, scalar2=None, op0=ALU.mult),
                  reads=wk, writes=wk)
    lb_compute(l0c, l1c, s0c, s1c, lbcol, omlc, nomlc, flagc, [("prm",)], [("lbc",)])

    for br in ("gla", "ret", "hg"):
        p = BR[br]
        KC, VC = p["KC"], p["VC"]
        DK, DV = KC * 128, VC * 128
        nh = p["nh"]
        XW = KC * DV + KC
        if not full:
            xs_d = C.dint(f"xs_{L}_{br}", [3 * nh * 128, XW])
            xd_d = C.dint(f"xd_{L}_{br}", [3 * nh * 128, XW])
        else:
            xd_d = C.dram[f"xd_{L}_{br}"]
        if br == "ret":
            rotF = C.din("rotF", [128, 2 * T]) if "rotF" not in C.dram else C.dram["rotF"]
            rotT = C.din("rotT", [128, 2 * 9 * 128]) if "rotT" not in C.dram else C.dram["rotT"]
            S.add("sp", lambda e: e.dma_start(out=tbl[:, 0:2 * T], in_=rotF), writes=[("tbl",)], dma=("tbl", 0))
            S.add("sp", lambda e: e.dma_start(out=tbl[:, 2 * T:2 * T + 2304], in_=rotT), writes=[("tbl",)], dma=("tbl", 1))
            cosF, sinF = tbl[:, 0:T], tbl[:, T:2 * T]
            cosT = tbl[:, 2 * T:2 * T + 1152].rearrange("p (a b) -> p a b", a=9)
            sinT = tbl[:, 2 * T + 1152:2 * T + 2304].rearrange("p (a b) -> p a b", a=9)
        if br == "hg":
            lbl = C.din("lbl", [128, 2048]) if "lbl" not in C.dram else C.dram["lbl"]
            S.add("sp", lambda e: e.dma_start(out=tbl[:, 2048:4096], in_=lbl), writes=[("tbl",)], dma=("tbl", 0))
            lbb, omlb = tbl[:, 0:1024], tbl[:, 1024:2048]
            lb_compute(tbl[:, 2048:3072], tbl[:, 3072:4096], tbl[:, 2048:3072], tbl[:, 3072:4096],
                       lbb, omlb, None, flagc[:, 0:1], [("prm",)], [("tbl",)])
        for h in range(nh):
            mix_head(C, B, wsl, br, h, full, locals())
    if not full:
        rg = [[0, 1, 2, 3], [4, 5, 6, 7]]
        for br in ("gla", "ret", "hg"):
            nh = BR[br]["nh"]
            xs_d, xd_d = C.dram[f"xs_{L}_{br}"], C.dram[f"xd_{L}_{br}"]
            S.add("pool", lambda e, xs_d=xs_d, xd_d=xd_d: e.collective_compute(
                "AllReduce", ALU.add, replica_groups=rg, ins=[xs_d.opt()], outs=[xd_d.opt()]),
                reads=[("xs", L, br, h, s) for h in range(nh) for s in range(3)], writes=[("xd", L, br)],
                dma=("cc", L, br), inc=1)
    S.barrier()


def mix_head(C, B, wsl, br, h, full, env):
    S = C.S
    ps = C.psum
    p = BR[br]
    KC, VC = p["KC"], p["VC"]
    DK, DV = KC * 128, VC * 128
    qs = p["qs"]
    tbl = env["tbl"]
    kcols = np.arange(p["k"] + h * DK, p["k"] + (h + 1) * DK)
    vcols = np.arange(p["v"] + h * DV, p["v"] + (h + 1) * DV)
    qcols = np.arange(p["q"] + h * DK, p["q"] + (h + 1) * DK)
    gcols = np.arange(p["g"] + h * DV, p["g"] + (h + 1) * DV)
    if br == "ret":
        cosF, sinF, cosT, sinT = env["cosF"], env["sinF"], env["cosT"], env["sinT"]
    if br == "hg":
        lbb, omlb = env["lbb"], env["omlb"]
        omlc, nomlc = env["omlc"], env["nomlc"]
    RT = [("tbl",)]

    def dve(fn, reads, writes):
        S.add("dve", fn, reads=reads, writes=writes)

    def act(fn, reads, writes):
        S.add("act", fn, reads=reads, writes=writes)

    def fm_tile(role, nch, ws, view, t0, n):
        if True:
            if True:
                bs_ = [proj_fm(C, ws, view, ch * 128, 128, t0, n) for ch in range(nch)]
                dst = {"q": B.qraw, "k": B.kraw, "g": B.sgT}[role]
                dkey = {"q": "qraw", "k": "kraw", "g": "sgT"}[role]
                if role == "g" or (role == "k" and br == "hg"):
                    for ch, b in enumerate(bs_):
                        act(lambda e, b=b, n=n: e.activation(out=B.fA[:, 0:n], in_=ps[:, b, 0:n], func=AF.Exp, scale=-1.0),
                            [("ps", b)], [("fA",)])
                        dve(lambda e, n=n: e.tensor_scalar(out=B.fA[:, 0:n], in0=B.fA[:, 0:n], scalar1=1.0, scalar2=None, op0=ALU.add),
                            [("fA",)], [("fA",)])
                        dve(lambda e, n=n: e.reciprocal(out=B.fA[:, 0:n], in_=B.fA[:, 0:n]), [("fA",)], [("fA",)])
                        if role == "g":
                            dve(lambda e, b=b, ch=ch, t0=t0, n=n: e.tensor_tensor(
                                out=dst[:, ch, t0:t0 + n], in0=B.fA[:, 0:n], in1=ps[:, b, 0:n], op=ALU.mult),
                                [("fA",), ("ps", b)], [(dkey, ch, t0)])
                        else:
                            dve(lambda e, ch=ch, t0=t0, n=n: e.tensor_scalar(
                                out=dst[:, ch, t0:t0 + n], in0=B.fA[:, 0:n], scalar1=nomlc[:, h:h + 1],
                                scalar2=omlc[:, h:h + 1], op0=ALU.mult, op1=ALU.add),
                                [("fA",), ("lbc",)], [(dkey, ch, t0)])
                elif br == "ret":
                    b0, b1 = bs_
                    sc = qs if role == "q" else 1.0
                    dve(lambda e, n=n, t0=t0: e.scalar_tensor_tensor(out=B.fA[:, 0:n], in0=ps[:, b0, 0:n], scalar=sc,
                        in1=cosF[:, t0:t0 + n], op0=ALU.mult, op1=ALU.mult), [("ps", b0)] + RT, [("fA",)])
                    dve(lambda e, n=n, t0=t0: e.scalar_tensor_tensor(out=B.fB[:, 0:n], in0=ps[:, b1, 0:n], scalar=sc,
                        in1=sinF[:, t0:t0 + n], op0=ALU.mult, op1=ALU.mult), [("ps", b1)] + RT, [("fB",)])
                    dve(lambda e, n=n, t0=t0: e.tensor_tensor(out=dst[:, 0, t0:t0 + n], in0=B.fA[:, 0:n], in1=B.fB[:, 0:n],
                        op=ALU.subtract), [("fA",), ("fB",)], [(dkey, 0, t0)])
                    dve(lambda e, n=n, t0=t0: e.scalar_tensor_tensor(out=B.fA[:, 0:n], in0=ps[:, b0, 0:n], scalar=sc,
                        in1=sinF[:, t0:t0 + n], op0=ALU.mult, op1=ALU.mult), [("ps", b0)] + RT, [("fA",)])
                    dve(lambda e, n=n, t0=t0: e.scalar_tensor_tensor(out=B.fB[:, 0:n], in0=ps[:, b1, 0:n], scalar=sc,
                        in1=cosF[:, t0:t0 + n], op0=ALU.mult, op1=ALU.mult), [("ps", b1)] + RT, [("fB",)])
                    dve(lambda e, n=n, t0=t0: e.tensor_tensor(out=dst[:, 1, t0:t0 + n], in0=B.fA[:, 0:n], in1=B.fB[:, 0:n],
                        op=ALU.add), [("fA",), ("fB",)], [(dkey, 1, t0)])
                else:
                    sc = qs if role == "q" else 1.0
                    b = bs_[0]
                    act(lambda e, b=b, t0=t0, n=n: e.activation(out=dst[:, 0, t0:t0 + n], in_=ps[:, b, 0:n],
                        func=AF.Copy, scale=sc), [("ps", b)], [(dkey, 0, t0)])

    if full:
        for role, cols, nch in (("q", qcols, KC), ("k", kcols, KC), ("g", gcols, VC)):
            ws, view = load_block(C, wsl, "w_in", NDC, cols)
            for (t0, n) in TT:
                fm_tile(role, nch, ws, view, t0, n)

    dve(lambda e: e.memset(B.Sst[:, 0:KC, 0:DV], 0.0), [], [("Sst",)])
    if not full:
        dve(lambda e: e.memset(B.dtot[:, 0:KC], 1.0), [], [("dtot",)])
    else:
        xd_d = env["xd_d"]
        nh_ = p["nh"]
        XW = KC * DV + KC
        dve(lambda e: e.memset(B.Sin[:, 0:KC, 0:DV], 0.0), [], [("Sin",)])
        for s in range(3):
            r0_ = (s * nh_ + h) * 128
            xi = C.rot("xst", 2)
            xst = B.xst[xi]
            XK = ("xst", xi)
            S.add("sp", lambda e, r0_=r0_, xst=xst: e.dma_start(out=xst[:, 0:XW], in_=xd_d[r0_:r0_ + 128, :]),
                  reads=[("xd", env["L"], br)], writes=[XK], dma=("xsw", xi))
            wcol = C.cm[:, 4 + s:5 + s]
            dve(lambda e, xst=xst: e.tensor_scalar(out=B.dpr[:, 0:KC], in0=xst[:, KC * DV:XW], scalar1=-1.0, scalar2=None,
                                                   op0=ALU.add), [XK], [("dpr",)])
            dve(lambda e, wcol=wcol: e.tensor_scalar(out=B.dpr[:, 0:KC], in0=B.dpr[:, 0:KC], scalar1=wcol, scalar2=None,
                                                     op0=ALU.mult), [("dpr",), ("cm",)], [("dpr",)])
            dve(lambda e: e.tensor_scalar(out=B.dpr[:, 0:KC], in0=B.dpr[:, 0:KC], scalar1=1.0, scalar2=None, op0=ALU.add),
                [("dpr",)], [("dpr",)])
            dve(lambda e, xst=xst, wcol=wcol: e.tensor_scalar(out=xst[:, 0:KC * DV], in0=xst[:, 0:KC * DV], scalar1=wcol,
                                                              scalar2=None, op0=ALU.mult), [XK, ("cm",)], [XK])
            for kc in range(KC):
                dve(lambda e, kc=kc, xst=xst: e.scalar_tensor_tensor(out=B.Sin[:, kc, 0:DV], in0=B.Sin[:, kc, 0:DV],
                    scalar=B.dpr[:, kc:kc + 1], in1=xst[:, kc * DV:(kc + 1) * DV], op0=ALU.mult, op1=ALU.add),
                    [("Sin",), ("dpr",), XK], [("Sin",)])
    if br == "ret":
        dve(lambda e: e.memset(B.gt[:, 0:DK], RET_LOGG[h]), [], [("gt",)])

    wsk, vk = load_block(C, wsl, "w_in", NDC, kcols)
    wsv, vv = load_block(C, wsl, "w_in", NDC, vcols)
    hc = slice(h * 128, (h + 1) * 128)

    def tile(ti, s0, n):
        ki = C.rot("khat", 2)
        vi = C.rot("vt", 2)
        khat, vt = B.khat[ki], B.vt[vi]
        KH, VT = ("khat", ki), ("vt", vi)
        bk = proj_tm(C, wsk, vk, DK, s0, n)
        bv = proj_tm(C, wsv, vv, DV, s0, n)
        if br == "gla":
            act(lambda e: e.activation(out=vt[0:n, 0:DV], in_=ps[0:n, bv, 0:DV], func=AF.Copy), [("ps", bv)], [VT])
            act(lambda e: e.activation(out=B.ktm[0:n, 0:DK], in_=ps[0:n, bk, 0:DK], func=AF.Copy), [("ps", bk)], [("ktm",)])
            bz = C.nextbank()
            S.add("pe", lambda e: e.matmul(ps[0:n, bz, 0:128], C.glr1T[0:17, s0:s0 + n], C.w2b[0:17, h * 128:(h + 1) * 128],
                                           start=True, stop=True), reads=[("glr",), ("w2b",)], writes=[("ps", bz)])
            act(lambda e: e.activation(out=B.e1[0:n, 0:128], in_=ps[0:n, bz, 0:128], func=AF.Exp, scale=-1.0),
                [("ps", bz)], [("e1",)])
            act(lambda e: e.activation(out=B.gt[0:n, 0:128], in_=B.e1[0:n, 0:128], func=AF.Ln, bias=C.onesf[0:n, 0:1], scale=1.0),
                [("e1",), ("onesf",)], [("gt",)])
            dve(lambda e: e.tensor_scalar(out=B.gt[0:n, 0:128], in0=B.gt[0:n, 0:128], scalar1=-1.0 / 16.0, scalar2=None,
                                          op0=ALU.mult), [("gt",)], [("gt",)])
        elif br == "ret":
            act(lambda e: e.activation(out=vt[0:n, 0:DV], in_=ps[0:n, bv, 0:DV], func=AF.Copy), [("ps", bv)], [VT])
            for half, (opA, tA, tB) in enumerate(((ALU.subtract, cosT, sinT), (ALU.add, sinT, cosT))):
                dve(lambda e, tA=tA: e.tensor_tensor(out=B.e1[0:n, 0:128], in0=ps[0:n, bk, 0:128], in1=tA[0:n, ti, :],
                                                    op=ALU.mult), [("ps", bk)] + RT, [("e1",)])
                dve(lambda e, tB=tB: e.tensor_tensor(out=B.e2[0:n, 0:128], in0=ps[0:n, bk, 128:256], in1=tB[0:n, ti, :],
                                                    op=ALU.mult), [("ps", bk)] + RT, [("e2",)])
                dve(lambda e, half=half, opA=opA: e.tensor_tensor(out=B.ktm[0:n, half * 128:(half + 1) * 128],
                                                                  in0=B.e1[0:n, 0:128], in1=B.e2[0:n, 0:128], op=opA),
                    [("e1",), ("e2",)], [("ktm",)])
        else:
            act(lambda e: e.activation(out=B.e1[0:n, 0:128], in_=ps[0:n, bk, 0:128], func=AF.Exp, scale=-1.0),
                [("ps", bk)], [("e1",)])
            dve(lambda e: e.tensor_scalar(out=B.e1[0:n, 0:128], in0=B.e1[0:n, 0:128], scalar1=1.0, scalar2=None, op0=ALU.add),
                [("e1",)], [("e1",)])
            dve(lambda e: e.reciprocal(out=B.e1[0:n, 0:128], in_=B.e1[0:n, 0:128]), [("e1",)], [("e1",)])
            dve(lambda e: e.tensor_tensor(out=B.e2[0:n, 0:128], in0=B.e1[0:n, 0:128], in1=omlb[0:n, hc], op=ALU.mult),
                [("e1",)] + RT, [("e2",)])
            dve(lambda e: e.scalar_tensor_tensor(out=B.gt[0:n, 0:128], in0=B.e2[0:n, 0:128], scalar=1e-20, in1=lbb[0:n, hc],
                                                 op0=ALU.max, op1=ALU.add), [("e2",)] + RT, [("gt",)])
            act(lambda e: e.activation(out=B.gt[0:n, 0:128], in_=B.gt[0:n, 0:128], func=AF.Ln), [("gt",)], [("gt",)])
            dve(lambda e: e.tensor_tensor(out=B.ktm[0:n, 0:128], in0=omlb[0:n, hc], in1=B.e2[0:n, 0:128], op=ALU.subtract),
                [("e2",)] + RT, [("ktm",)])
            act(lambda e: e.activation(out=B.eE[0:n, 0:128], in_=ps[0:n, bv, 0:128], func=AF.Exp, scale=-1.0),
                [("ps", bv)], [("eE",)])
            dve(lambda e: e.tensor_scalar(out=B.eE[0:n, 0:128], in0=B.eE[0:n, 0:128], scalar1=1.0, scalar2=None, op0=ALU.add),
                [("eE",)], [("eE",)])
            dve(lambda e: e.reciprocal(out=B.eE[0:n, 0:128], in_=B.eE[0:n, 0:128]), [("eE",)], [("eE",)])
            dve(lambda e: e.tensor_tensor(out=vt[0:n, 0:128], in0=B.eE[0:n, 0:128], in1=ps[0:n, bv, 0:128], op=ALU.mult),
                [("eE",), ("ps", bv)], [VT])
        bE = C.nextbank()
        S.add("pe", lambda e: e.matmul(ps[0:n, bE, 0:DK], C.cU[0:n, 0:n], B.gt[0:n, 0:DK], start=True, stop=True),
              reads=[("cU",), ("gt",)], writes=[("ps", bE)])
        act(lambda e: e.activation(out=B.eE[0:n, 0:DK], in_=ps[0:n, bE, 0:DK], func=AF.Exp), [("ps", bE)], [("eE",)])
        dve(lambda e: e.tensor_tensor(out=khat[0:n, 0:DK], in0=B.ktm[0:n, 0:DK], in1=B.eE[0:n, 0:DK], op=ALU.mult),
            [("ktm",), ("eE",)], [KH])
        Mq = C.cMq16 if n == 16 else C.cMq
        mqk = ("cMq16",) if n == 16 else ("cMq",)
        for kc in range(KC):
            bX = C.nextbank()
            S.add("pe", lambda e, kc=kc, bX=bX: e.matmul(ps[:, bX, 0:n + 4], B.gt[0:n, kc * 128:(kc + 1) * 128],
                                                         Mq[0:n, 0:n + 4], start=True, stop=True),
                  reads=[("gt",), mqk], writes=[("ps", bX)])
            act(lambda e, kc=kc, bX=bX: e.activation(out=B.decs[:, kc, 0:4], in_=ps[:, bX, n:n + 4], func=AF.Exp),
                [("ps", bX)], [("decs",)])
            if full:
                act(lambda e, kc=kc, bX=bX: e.activation(out=B.eq[:, kc, 0:n], in_=ps[:, bX, 0:n], func=AF.Exp),
                    [("ps", bX)], [("eq",)])
                act(lambda e, kc=kc, bX=bX: e.activation(out=B.ek[:, kc, 0:n], in_=ps[:, bX, 0:n], func=AF.Exp, scale=-1.0),
                    [("ps", bX)], [("ek",)])
        if full:
            qi = C.rot("qt", 2)
            pi = C.rot("Pm", 2)
            qt, kt, Pm = B.qt[qi], B.kt[qi], B.Pm[pi]
            tk = ukey(s0)
            dve(lambda e: e.tensor_tensor(out=qt[:, 0:KC, 0:n], in0=B.qraw[:, 0:KC, s0:s0 + n], in1=B.eq[:, 0:KC, 0:n], op=ALU.mult),
                [("qraw", kc, tk) for kc in range(KC)] + [("eq",)], [("qt", qi)])
            dve(lambda e: e.tensor_tensor(out=kt[:, 0:KC, 0:n], in0=B.kraw[:, 0:KC, s0:s0 + n], in1=B.ek[:, 0:KC, 0:n], op=ALU.mult),
                [("kraw", kc, tk) for kc in range(KC)] + [("ek",)], [("kt", qi)])
            bs = C.nextbank()
            for kc in range(KC):
                S.add("pe", lambda e, kc=kc: e.matmul(ps[0:n, bs, 0:n], kt[:, kc, 0:n], qt[:, kc, 0:n],
                                                     start=(kc == 0), stop=(kc == KC - 1)),
                      reads=[("kt", qi), ("qt", qi)], writes=[("ps", bs)])
            dve(lambda e: e.tensor_tensor(out=Pm[0:n, 0:n], in0=ps[0:n, bs, 0:n], in1=C.cmask[0:n, 0:n], op=ALU.mult),
                [("ps", bs), ("cmask",)], [("Pm", pi)])
        chunks = [(0, min(n, 64))] + ([(64, 64)] if n == 128 else [])
        for ci, (r0, rn) in enumerate(chunks):
            if full:
                for kc in range(KC):
                    dve(lambda e, kc=kc, ci=ci: e.tensor_scalar(out=B.Sb[:, ci * 2 + kc, 0:DV], in0=B.Sst[:, kc, 0:DV],
                        scalar1=B.decs[:, kc, 2 + ci:3 + ci], scalar2=None, op0=ALU.mult),
                        [("Sst",), ("decs",)], [("Sb", ci)])
            for kc in range(KC):
                bS = C.nextbank()
                S.add("pe", lambda e, kc=kc, bS=bS, r0=r0, rn=rn: e.matmul(
                    ps[:, bS, 0:DV], khat[r0:r0 + rn, kc * 128:(kc + 1) * 128], vt[r0:r0 + rn, 0:DV], start=True, stop=True),
                    reads=[KH, VT], writes=[("ps", bS)])
                dve(lambda e, kc=kc, bS=bS, ci=ci: e.scalar_tensor_tensor(
                    out=B.Sst[:, kc, 0:DV], in0=B.Sst[:, kc, 0:DV], scalar=B.decs[:, kc, ci:ci + 1], in1=ps[:, bS, 0:DV],
                    op0=ALU.mult, op1=ALU.add), [("Sst",), ("decs",), ("ps", bS)], [("Sst",)])
                if (not full) and ti >= 1:
                    dve(lambda e, kc=kc, ci=ci: e.tensor_tensor(out=B.dtot[:, kc:kc + 1], in0=B.dtot[:, kc:kc + 1],
                        in1=B.decs[:, kc, ci:ci + 1], op=ALU.mult), [("dtot",), ("decs",)], [("dtot",)])
        if full:
            for vc in range(VC):
                bo = C.nextbank()
                S.add("pe", lambda e, vc=vc, bo=bo: e.matmul(ps[:, bo, 0:n], vt[0:n, vc * 128:(vc + 1) * 128], Pm[0:n, 0:n],
                                                           start=True, stop=False),
                      reads=[VT, ("Pm", pi)], writes=[("ps", bo)])
                last = (len(chunks) - 1, KC - 1)
                for ci, (r0, rn) in enumerate(chunks):
                    for kc in range(KC):
                        S.add("pe", lambda e, vc=vc, bo=bo, ci=ci, kc=kc, r0=r0, rn=rn: e.matmul(
                            ps[:, bo, r0:r0 + rn], B.Sb[:, ci * 2 + kc, vc * 128:(vc + 1) * 128], qt[:, kc, r0:r0 + rn],
                            start=False, stop=((ci, kc) == last)),
                            reads=[("Sb", ci), ("qt", qi)], writes=[("ps", bo)])
                act(lambda e, vc=vc, bo=bo: e.activation(out=B.osb[:, vc, 0:n], in_=ps[:, bo, 0:n], func=AF.Copy),
                    [("ps", bo)], [("osb", vc)])
            OS = [("osb", vc) for vc in range(VC)]
            if br == "ret":
                bm = C.nextbank()
                for vc in range(VC):
                    S.add("pe", lambda e, vc=vc: e.matmul(ps[:, bm, 0:n], C.onesf[:], B.osb[:, vc, 0:n],
                                                         start=(vc == 0), stop=(vc == VC - 1)),
                          reads=[("onesf",), ("osb", vc)], writes=[("ps", bm)])
                for vc in range(VC):
                    dve(lambda e, vc=vc: e.scalar_tensor_tensor(out=B.osb[:, vc, 0:n], in0=ps[:, bm, 0:n], scalar=-1.0 / DV,
                        in1=B.osb[:, vc, 0:n], op0=ALU.mult, op1=ALU.add), [("ps", bm), ("osb", vc)], [("osb", vc)])
            bq = C.nextbank()
            for vc in range(VC):
                act(lambda e, vc=vc: e.activation(out=B.sqo[:, vc, 0:n], in_=B.osb[:, vc, 0:n], func=AF.Square),
                    [("osb", vc)], [("sqo", vc)])
                S.add("pe", lambda e, vc=vc: e.matmul(ps[:, bq, 0:n], C.ones[:], B.sqo[:, vc, 0:n],
                                                     start=(vc == 0), stop=(vc == VC - 1)),
                      reads=[("ones",), ("sqo", vc)], writes=[("ps", bq)])
            rstd_from_ps(C, bq, n, B.rs[:, 0:n], ("rs",), 1.0 / DV)
            yi = C.rot("yst", 3)
            yst = B.yst[yi]
            for vc in range(VC):
                gcol = C.pb + p["ncol"] + h * VC + vc
                dve(lambda e, vc=vc, gcol=gcol: e.scalar_tensor_tensor(out=B.osb[:, vc, 0:n], in0=B.osb[:, vc, 0:n],
                    scalar=C.prm[:, gcol:gcol + 1], in1=B.rs[:, 0:n], op0=ALU.mult, op1=ALU.mult),
                    [("osb", vc), ("prm",), ("rs",)], [("osb", vc)])
                dve(lambda e, vc=vc: e.tensor_tensor(out=yst[:, vc, 0:n], in0=B.osb[:, vc, 0:n], in1=B.sgT[:, vc, s0:s0 + n],
                    op=ALU.mult), [("osb", vc), ("sgT", vc, ukey(s0))], [("yst", yi)])
            ch0 = p["ys0"] + h * VC
            ysd = C.dram["ysd"]
            S.add("sp", lambda e: e.dma_start(out=ysd[ch0:ch0 + VC, :, s0:s0 + n].rearrange("c p t -> p c t"),
                                              in_=yst[:, 0:VC, 0:n]),
                  reads=[("yst", yi)], writes=[("ysd", ch0 + vc, ti) for vc in range(VC)], dma=("ysw", yi))
            if ti == 0:
                dve(lambda e: e.tensor_tensor(out=B.Sst[:, 0:KC, 0:DV], in0=B.Sst[:, 0:KC, 0:DV], in1=B.Sin[:, 0:KC, 0:DV],
                                              op=ALU.add), [("Sst",), ("Sin",)], [("Sst",)])
    for ti, (s0, n) in enumerate(TM):
        tile(ti, s0, n)
    if not full:
        xs_d = env["xs_d"]
        nh_ = p["nh"]
        XW = KC * DV + KC
        for s in range(3):
            r0_ = (s * nh_ + h) * 128
            xi = C.rot("xst", 2)
            xst = B.xst[xi]
            XK = ("xst", xi)
            mcol = C.cm[:, s:s + 1]
            dve(lambda e, xst=xst, mcol=mcol: e.tensor_scalar(
                out=xst[:, 0:KC * DV].rearrange("p (k v) -> p k v", k=KC), in0=B.Sst[:, 0:KC, 0:DV], scalar1=mcol,
                scalar2=None, op0=ALU.mult), [("Sst",), ("cm",)], [XK])
            dve(lambda e, xst=xst, mcol=mcol: e.tensor_scalar(out=xst[:, KC * DV:XW], in0=B.dtot[:, 0:KC], scalar1=mcol,
                                                              scalar2=None, op0=ALU.mult), [("dtot",), ("cm",), XK], [XK])
            S.add("sp", lambda e, r0_=r0_, xst=xst: e.dma_start(out=xs_d[r0_:r0_ + 128, :], in_=xst[:, 0:XW]),
                  reads=[XK], writes=[("xs", env["L"], br, h, s)], dma=("xsw", xi))

def st_merge(C):
    S = C.S
    ps = C.psum
    S.barrier()
    A = C.Arena()
    ysT = A.bf16([24, T])
    mergedT = A.bf16([NDC, T])
    wsl = [A.bf16([2048]) for _ in range(3)]
    fS = [A.f32([512]) for _ in range(2)]
    acc = C.rstd
    ysd = C.dram["ysd"]
    for q in range(3):
        S.add("sp", lambda e, q=q: e.dma_start(out=ysT[:, 8 * q:8 * q + 8, :],
                                               in_=ysd[8 * q:8 * q + 8].rearrange("c p t -> p c t")),
              reads=[("ysd", c, ti) for c in range(8 * q, 8 * q + 8) for ti in range(9)],
              writes=[("ysT", q)], dma=("ysr", q))

    def merge_tile(dc, nb, wsb, wbv, wsm, mgv, t0, n):
        bz = C.nextbank()
        for wc in range(8):
            S.add("pe", lambda e, wc=wc: e.matmul(ps[:, bz, 0:n], wbv[:, wc, :], ysT[:, nb * 8 + wc, t0:t0 + n],
                                                 start=(wc == 0), stop=(wc == 7)),
                  reads=[("wsl", wsb), ("ysT", nb)], writes=[("ps", bz)])
        bg = proj_fm(C, wsm, mgv, 0, 128, t0, n)
        fi = C.rot("fS", 2)
        f = fS[fi]
        S.add("act", lambda e: e.activation(out=f[:, 0:n], in_=ps[:, bg, 0:n], func=AF.Exp, scale=-1.0),
              reads=[("ps", bg)], writes=[("fS", fi)])
        S.add("dve", lambda e: e.tensor_scalar(out=f[:, 0:n], in0=f[:, 0:n], scalar1=1.0, scalar2=None, op0=ALU.add),
              reads=[("fS", fi)], writes=[("fS", fi)])
        S.add("dve", lambda e: e.reciprocal(out=f[:, 0:n], in_=f[:, 0:n]), reads=[("fS", fi)], writes=[("fS", fi)])
        if nb == 0:
            S.add("dve", lambda e: e.tensor_tensor(out=acc[:, t0:t0 + n], in0=f[:, 0:n], in1=ps[:, bz, 0:n], op=ALU.mult),
                  reads=[("fS", fi), ("ps", bz)], writes=[("acc", t0)])
        else:
            S.add("dve", lambda e: e.tensor_tensor(out=f[:, 0:n], in0=f[:, 0:n], in1=ps[:, bz, 0:n], op=ALU.mult),
                  reads=[("fS", fi), ("ps", bz)], writes=[("fS", fi)])
            if nb == 1:
                S.add("dve", lambda e: e.tensor_tensor(out=acc[:, t0:t0 + n], in0=acc[:, t0:t0 + n], in1=f[:, 0:n], op=ALU.add),
                      reads=[("fS", fi), ("acc", t0)], writes=[("acc", t0)])
            else:
                S.add("dve", lambda e: e.tensor_tensor(out=mergedT[:, dc, t0:t0 + n], in0=acc[:, t0:t0 + n], in1=f[:, 0:n],
                                                       op=ALU.add),
                      reads=[("fS", fi), ("acc", t0)], writes=[("mT", dc, t0)])

    for dc in range(NDC):
        for nb in range(3):
            wsb, wbv = load_block(C, wsl, "wb", 8, np.arange(dc * 128, (dc + 1) * 128) + 0, cap=2048) if False else \
                load_block(C, wsl, ("wb", nb), 8, np.arange(dc * 128, (dc + 1) * 128), cap=2048)
            wsm, mgv = load_block(C, wsl, "w_in", NDC,
                                  np.arange(MG + nb * 2048 + dc * 128, MG + nb * 2048 + (dc + 1) * 128), cap=2048)
            for (t0, n) in TT:
                merge_tile(dc, nb, wsb, wbv, wsm, mgv, t0, n)

    def out_tile(ws, wv, dd, dco, t0, n):
        b = C.nextbank()
        for c in range(NDC):
            S.add("pe", lambda e, c=c: e.matmul(ps[:, b, 0:n], wv[:, c, dd * 128:(dd + 1) * 128], mergedT[:, c, t0:t0 + n],
                                                start=(c == 0), stop=(c == NDC - 1)),
                  reads=[("wsl", ws), ("mT", c, t0)], writes=[("ps", b)])
        S.add("dve", lambda e: e.tensor_tensor(out=C.hT[:, dco, t0:t0 + n], in0=ps[:, b, 0:n], in1=C.hT[:, dco, t0:t0 + n],
                                               op=ALU.add),
              reads=[("ps", b), ("hT", dco)], writes=[("hT", dco)])

    for dco in range(NDC):
        ws, wv = load_block(C, wsl, "wout", NDC, np.arange(dco * 128, (dco + 1) * 128), cap=2048)
        for (t0, n) in TT:
            out_tile(ws, wv, 0, dco, t0, n)
    S.barrier()


def lay_wgu(wg, wu):
    out = np.empty((NFC, 128, 2, NDC, 128), np.float32)
    out[:, :, 0] = wg.reshape(NDC, 128, NFC, 128).transpose(2, 1, 0, 3)
    out[:, :, 1] = wu.reshape(NDC, 128, NFC, 128).transpose(2, 1, 0, 3)
    return out.reshape(NFC, 128, 2 * NDC * 128)


def lay_wd(wd):
    out = np.zeros((4, 8, 128, GF, 256), np.float32)
    w4 = wd.reshape(NFC, 128, 8, 256)
    for gi, (f0, nf) in enumerate(FGROUPS):
        out[gi, :, :, :nf, :] = w4[f0:f0 + nf].transpose(2, 1, 0, 3)
    return out.reshape(4, 8, 128, GF * 256)


def col_lay(v):
    n = v.shape[0] // 128
    return np.ascontiguousarray(v.reshape(n, 128).T)


def make_wblk(blocks, srcs):
    parts = []
    for (src, RC, cols) in blocks:
        M = srcs[src]
        blk = M[:, cols].reshape(RC, 128, len(cols)).transpose(1, 0, 2).reshape(128, RC * len(cols))
        parts.append(blk)
    return np.ascontiguousarray(np.concatenate(parts, axis=1))


def host_consts():
    j = np.arange(128)
    ch = j // 64
    same = ch[:, None] == ch[None, :]
    tri = (j[:, None] <= j[None, :]) & same
    mid = ch * 64 + 31
    trimid = (j[:, None] <= mid[None, :]) & same
    Mq = tri.astype(np.float32) - trimid.astype(np.float32)
    CI = np.stack([(ch == 0), (ch == 1), (ch == 0) & (j <= 31), (ch == 1) & (j <= 95)], axis=1).astype(np.float32)
    U = ((j[:, None] > j[None, :]) & same).astype(np.float32)
    mask = tri.astype(np.float32)
    cst = np.zeros((128, 408), np.float32)
    cst[:, 0:128] = U
    cst[:, 128:256] = Mq
    cst[:, 256:260] = CI
    Mq16 = np.zeros((128, 20), np.float32)
    Mq16[:16, :16] = tri[:16, :16].astype(np.float32) - 1.0
    Mq16[:16, 16] = 1.0
    Mq16[:16, 18] = 1.0
    cst[:, 260:280] = Mq16
    cst[:, 280:408] = mask
    return cst


def rot_tables(jseg):
    pos = np.concatenate([np.arange(16), 16 + jseg * 1024 + np.arange(1024)]).astype(np.float32)
    half = 128
    inv_freq = (np.float32(10000.0) ** (-np.arange(half, dtype=np.float32) / np.float32(half))).astype(np.float32)
    ang = (pos[:, None] * inv_freq[None, :]).astype(np.float32)
    cos, sin = np.cos(ang).astype(np.float32), np.sin(ang).astype(np.float32)
    rotF = np.concatenate([cos.T, sin.T], axis=1)
    rt = np.zeros((2, 128, 9, 128), np.float32)
    for k, tab in enumerate((cos, sin)):
        rt[k, :16, 0, :] = tab[0:16]
        rt[k, :, 1:, :] = tab[16:].reshape(8, 128, 128).transpose(1, 0, 2)
    rotT = np.concatenate([rt[0].reshape(128, 1152), rt[1].reshape(128, 1152)], axis=1)
    return np.ascontiguousarray(rotF), np.ascontiguousarray(rotT)


_PROGS = {}


def get_prog():
    if "F" not in _PROGS:
        st = [(st_load_h, "hin")]
        for L in range(2):
            st += [(st_set_layer, L), (st_ffn, f"wgu{L}a", f"wd{L}a", P_FFN1), (st_mixer, "state"),
                   (st_mixer, "full"), (st_merge,), (st_ffn, f"wgu{L}b", f"wd{L}b", P_FFN2)]
        st += [(st_final_norm,), (st_store_h, "hfin")]
        _PROGS["F"] = build_program(st)
    return _PROGS["F"]


def kernel(x, meta_tokens, ffn1_norm, ffn1_w_gate, ffn1_w_up, ffn1_w_down, mix_norm, w_in,
           gla_w_gate2, gla_b_gate, gla_norm, ret_norm, hgrn_lb_logits, hgrn_norm, w_branch, w_out,
           ffn2_norm, ffn2_w_gate, ffn2_w_up, ffn2_w_down, final_norm):
    f = lambda a: np.asarray(a, dtype=np.float32)
    x = f(x)
    ncores = 8
    nc, C = get_prog()
    shared = {"cst": host_consts()}
    prm = np.zeros((128, 512), np.float32)
    srcs = {}
    for layer in range(2):
        pb = 256 * layer
        prm[:, pb + P_FFN1:pb + P_FFN1 + 16] = col_lay(f(ffn1_norm)[layer])
        prm[:, pb + P_MIX:pb + P_MIX + 16] = col_lay(f(mix_norm)[layer])
        prm[:, pb + P_FFN2:pb + P_FFN2 + 16] = col_lay(f(ffn2_norm)[layer])
        prm[:, pb + 64:pb + 72] = col_lay(f(gla_norm)[layer])
        prm[:, pb + 72:pb + 80] = col_lay(f(ret_norm)[layer])
        prm[:, pb + 80:pb + 88] = col_lay(f(hgrn_norm)[layer])
        prm[:, pb + P_LB0:pb + P_LB0 + 8] = col_lay(f(hgrn_lb_logits)[0])
        prm[:, pb + P_LB1:pb + P_LB1 + 8] = col_lay(f(hgrn_lb_logits)[1])
        prm[:, pb + 104] = float(layer)
        shared[f"w2b{layer}"] = np.concatenate([f(gla_w_gate2)[layer], f(gla_b_gate)[layer][None, :]], axis=0)
        srcs[("w_in", layer)] = f(w_in)[layer]
        srcs[("wout", layer)] = f(w_out)[layer]
        for nb in range(3):
            srcs[(("wb", nb), layer)] = f(w_branch)[layer, nb]
        shared[f"wgu{layer}a"] = lay_wgu(f(ffn1_w_gate)[layer], f(ffn1_w_up)[layer])
        shared[f"wd{layer}a"] = lay_wd(f(ffn1_w_down)[layer])
        shared[f"wgu{layer}b"] = lay_wgu(f(ffn2_w_gate)[layer], f(ffn2_w_up)[layer])
        shared[f"wd{layer}b"] = lay_wd(f(ffn2_w_down)[layer])
    prm[:, P_FIN:P_FIN + 16] = col_lay(f(final_norm))
    shared["prm"] = prm
    shared["wblk"] = make_wblk(C.blocks, srcs)
    shared["lbl"] = np.ascontiguousarray(np.broadcast_to(f(hgrn_lb_logits).reshape(1, 2048), (128, 2048)))
    rots = [rot_tables(j) for j in range(4)]
    maps = []
    for c in range(ncores):
        b, j = c // 4, c % 4
        h0 = np.zeros((T, D), np.float32)
        if j == 0:
            h0[0:16] = f(meta_tokens)
        h0[16:] = x[b, j * 1024:(j + 1) * 1024]
        cm = np.zeros((128, 8), np.float32)
        for s in range(3):
            cm[:, s] = 1.0 if s == j else 0.0
            cm[:, 4 + s] = 1.0 if s < j else 0.0
        m = dict(shared)
        m.update({"hin": np.ascontiguousarray(h0.T), "cm": cm, "rotF": rots[j][0], "rotT": rots[j][1]})
        maps.append(m)
    res = run_bass_kernel_spmd(nc, maps, core_ids=list(range(ncores))).results
    out = np.empty((2, 4096, D), np.float32)
    for c in range(ncores):
        b, j = c // 4, c % 4
        out[b, j * 1024:(j + 1) * 1024] = res[c]["hfin"][:, 16:].T
    return out
```

```python
import numpy as np
import concourse.bass as bass
import concourse.mybir as mybir
from concourse.bass_utils import run_bass_kernel_spmd

F32 = mybir.dt.float32
BF16 = mybir.dt.bfloat16
AF = mybir.ActivationFunctionType
ALU = mybir.AluOpType

D = 2048
NDC = 16
T = 1040
NMETA = 16
DFF = 5504
NFC = 43
EPS = 1e-6
TT = [(16, 512), (528, 512), (0, 16)]
FGROUPS = [(0, 11), (11, 11), (22, 11), (33, 10)]
GF = 11


class Op:
    __slots__ = ("eng", "fn", "deps", "is_dma", "semkey", "needs_signal", "sig", "dma_ord", "gi", "inc")


class Sched:
    ENGS = ("pe", "act", "dve", "pool", "sp")

    def __init__(self):
        self.ops = []
        self.last_writer = {}
        self.readers = {}
        self.dma_count = {}
        self.out_dmas = []
        self.extra = []
        self.last_eng = {}
        self.last_dma = {}

    def add(self, eng, fn, reads=(), writes=(), dma=None, is_out=False, inc=16):
        op = Op()
        op.inc = inc
        op.eng = eng
        op.fn = fn
        op.is_dma = dma is not None
        op.semkey = dma
        op.needs_signal = False
        op.sig = None
        op.gi = len(self.ops)
        deps = set()
        for k in reads:
            w = self.last_writer.get(k)
            if w is not None:
                deps.add(w)
        for k in writes:
            w = self.last_writer.get(k)
            if w is not None:
                deps.add(w)
            for r in self.readers.get(k, ()):
                deps.add(r)
        deps.update(self.extra)
        deps.discard(op)
        op.deps = deps
        if dma is None:
            self.last_eng[eng] = op
        else:
            self.last_dma[dma] = op
        for k in writes:
            self.last_writer[k] = op
            self.readers[k] = []
        for k in reads:
            self.readers.setdefault(k, []).append(op)
        if op.is_dma:
            n = self.dma_count.get(dma, 0) + 1
            self.dma_count[dma] = n
            op.dma_ord = n
            if is_out:
                self.out_dmas.append(op)
        self.ops.append(op)
        return op

    def barrier(self):
        self.extra = list(self.last_eng.values()) + [v for k, v in self.last_dma.items() if k[0] != "cc"]

    def emit(self, nc, block_engines, sems):
        per_eng = {e: [] for e in self.ENGS}
        for op in self.ops:
            per_eng[op.eng].append(op)
        for op in self.ops:
            best = {}
            for d in op.deps:
                if d.is_dma:
                    k = ("dma", d.semkey)
                else:
                    if d.eng == "pe" and op.eng == "pe" and not op.is_dma:
                        continue
                    k = ("eng", d.eng)
                o = best.get(k)
                if o is None or d.gi > o.gi:
                    best[k] = d
            op.deps = list(best.values())
        for op in self.ops:
            for d in op.deps:
                if not d.is_dma:
                    d.needs_signal = True
        for e in self.ENGS:
            n = 0
            for op in per_eng[e]:
                if op.is_dma:
                    continue
                if op.needs_signal:
                    n += 1
                    op.sig = n
        self.sig_totals = {}
        for e in self.ENGS:
            self.sig_totals[e] = sum(1 for op in per_eng[e] if (not op.is_dma and op.needs_signal))

        def run_engine(ename, eng):
            waited = {}
            for op in per_eng[ename]:
                need = {}
                for d in op.deps:
                    if d.is_dma:
                        key = ("dma", d.semkey)
                        val = d.inc * d.dma_ord
                    else:
                        if d.eng == "pe" and ename == "pe" and not op.is_dma:
                            continue
                        key = ("eng", d.eng)
                        val = d.sig
                    if val > need.get(key, 0):
                        need[key] = val
                for key, val in need.items():
                    if waited.get(key, 0) >= val:
                        continue
                    eng.wait_ge(sems[key], val)
                    waited[key] = val
                ins = op.fn(eng)
                if op.is_dma:
                    ins.then_inc(sems[("dma", op.semkey)], op.inc)
                elif op.needs_signal:
                    ins.then_inc(sems[("eng", ename)], 1)
            if ename == "sp":
                final = {}
                for op in self.out_dmas:
                    key = ("dma", op.semkey)
                    final[key] = max(final.get(key, 0), 16 * self.dma_count[op.semkey])
                for key, val in final.items():
                    eng.wait_ge(sems[key], val)

        for ename, reg in block_engines.items():
            def mk(ename=ename):
                def f(eng):
                    run_engine(ename, eng)
                return f
            reg(mk())


GQ, GK, GV, GG, GLR, RQ, RK, RV, RG, HQ, HF, HI, HGC, MG = (
    0, 512, 1024, 2048, 3072, 3088, 4112, 5136, 6160, 7184, 8208, 9232, 10256, 11280)
BR = {
    "gla": dict(nh=4, KC=1, VC=2, q=GQ, k=GK, v=GV, g=GG, qs=128 ** -0.5, ys0=0, ncol=64),
    "ret": dict(nh=4, KC=2, VC=2, q=RQ, k=RK, v=RV, g=RG, qs=256 ** -0.5, ys0=8, ncol=72),
    "hg": dict(nh=8, KC=1, VC=1, q=HQ, k=HF, v=HI, g=HGC, qs=1.0, ys0=16, ncol=80),
}
TM = [(0, 16)] + [(16 + 128 * i, 128) for i in range(8)]
ARENA_W = 26000
P_FFN1, P_MIX, P_FFN2, P_FIN, P_LB0, P_LB1 = 0, 16, 32, 48, 88, 96
RET_LOGG = [float(np.log(1.0 - 2.0 ** (-5.0 - h))) for h in range(4)]


def ukey(tok0):
    return 0 if tok0 < 16 else (16 if tok0 < 528 else 528)


class Ctx:
    pass


def build_program(stages):
    nc = bass.Bass("TRN2", target_bir_lowering=False)
    S = Sched()
    C = Ctx()
    C.nc, C.S = nc, S
    C.dram = {}
    C.blocks = []
    C.blk_off = 0

    def din(name, shape, dt=F32):
        C.dram[name] = nc.dram_tensor(name, list(shape), dt, kind="ExternalInput").ap()
        return C.dram[name]

    def dout(name, shape, dt=F32):
        C.dram[name] = nc.dram_tensor(name, list(shape), dt, kind="ExternalOutput").ap()
        return C.dram[name]

    def dint(name, shape, dt=F32):
        C.dram[name] = nc.dram_tensor(name, list(shape), dt, kind="Internal").ap()
        return C.dram[name]

    C.din, C.dout, C.dint = din, dout, dint
    from contextlib import ExitStack
    with ExitStack() as es:
        def sb(name, shape, dt):
            return es.enter_context(nc.sbuf_tensor("sb_" + name, list(shape), dt))
        C.sb = sb
        C.hT = sb("hT", [128, NDC, T], F32)
        C.uT = sb("uT", [128, NDC, T], BF16)
        C.rstd = sb("rstd", [128, T], F32)
        C.ones = sb("ones", [128, 128], BF16)
        C.onesf = sb("onesf", [128, 128], F32)
        C.prm = sb("prm", [128, 512], F32)
        C.cm = sb("cm", [128, 8], F32)
        C.pb = 0
        C.L = 0
        C.epsb = sb("epsb", [128, 2], F32)
        C.cU = sb("cU", [128, 128], F32)
        C.cMq = sb("cMq", [128, 132], F32)
        C.cMq16 = sb("cMq16", [128, 20], F32)
        C.cmask = sb("cmask", [128, 128], F32)
        C.arena = sb("arena", [128, ARENA_W], F32)
        C.psum = es.enter_context(nc.psum_tensor("psum_all", [128, 8, 512], F32))
        C.bank_i = 0
        C.rr = {}

        def nextbank():
            b = C.bank_i % 8
            C.bank_i += 1
            return b
        C.nextbank = nextbank

        def rot(name, n):
            i = C.rr.get(name, 0)
            C.rr[name] = i + 1
            return i % n
        C.rot = rot

        class Arena:
            def __init__(self):
                self.off = 0

            def f32(self, shape):
                n = int(np.prod(shape))
                v = C.arena[:, self.off:self.off + n]
                self.off += n
                assert self.off <= ARENA_W, self.off
                if len(shape) == 2:
                    return v.rearrange("p (a b) -> p a b", a=shape[0])
                if len(shape) == 3:
                    return v.rearrange("p (a b c) -> p a b c", a=shape[0], b=shape[1])
                return v

            def bf16(self, shape):
                n = int(np.prod(shape))
                w = (n + 1) // 2
                v = C.arena[:, self.off:self.off + w].bitcast(BF16)
                self.off += w
                assert self.off <= ARENA_W, self.off
                if len(shape) == 2:
                    return v.rearrange("p (a b) -> p a b", a=shape[0])
                if len(shape) == 3:
                    return v.rearrange("p (a b c) -> p a b c", a=shape[0], b=shape[1])
                return v
        C.Arena = Arena

        S.add("dve", lambda e: e.memset(C.ones[:], 1.0), writes=[("ones",)])
        S.add("dve", lambda e: e.memset(C.onesf[:], 1.0), writes=[("onesf",)])
        S.add("dve", lambda e: e.memset(C.epsb[:], EPS), writes=[("epsb",)])
        cst = din("cst", [128, 408])
        S.add("sp", lambda e: e.dma_start(out=C.cU[:], in_=cst[:, 0:128]), writes=[("cU",)], dma=("c", 0))
        S.add("sp", lambda e: e.dma_start(out=C.cMq[:], in_=cst[:, 128:260]), writes=[("cMq",)], dma=("c", 1))
        S.add("sp", lambda e: e.dma_start(out=C.cMq16[:], in_=cst[:, 260:280]), writes=[("cMq16",)], dma=("c", 2))
        S.add("sp", lambda e: e.dma_start(out=C.cmask[:], in_=cst[:, 280:408]), writes=[("cmask",)], dma=("c", 3))
        prm = din("prm", [128, 512])
        S.add("sp", lambda e: e.dma_start(out=C.prm[:], in_=prm), writes=[("prm",)], dma=("c", 4))
        cmd = din("cm", [128, 8])
        S.add("sp", lambda e: e.dma_start(out=C.cm[:], in_=cmd), writes=[("cm",)], dma=("c", 5))

        for st in stages:
            st[0](C, *st[1:])

        if C.blk_off > 0:
            din("wblk", [128, C.blk_off])
        sems = {}
        for e in Sched.ENGS:
            sems[("eng", e)] = es.enter_context(nc.semaphore(f"s_{e}"))
        for i, k in enumerate(S.dma_count.keys()):
            sems[("dma", k)] = es.enter_context(nc.semaphore(f"d_{i}"))
        C.n_sems = len(sems)
        with nc.Block() as block:
            S.emit(nc, {"pe": block.tensor, "act": block.scalar, "dve": block.vector,
                        "pool": block.gpsimd, "sp": block.sync}, sems)
    return nc, C


def st_load_h(C, name):
    x = C.din(name, [D, T])
    xv = x.rearrange("(c p) t -> p c t", p=128)
    for q in range(4):
        cs = slice(4 * q, 4 * q + 4)
        C.S.add("sp", lambda e, cs=cs: e.dma_start(out=C.hT[:, cs, :], in_=xv[:, cs, :]),
                writes=[("hT", c) for c in range(4 * q, 4 * q + 4)], dma=("ldh", q))


def st_store_h(C, name):
    o = C.dout(name, [D, T])
    ov = o.rearrange("(c p) t -> p c t", p=128)
    for q in range(4):
        cs = slice(4 * q, 4 * q + 4)
        C.S.add("sp", lambda e, cs=cs: e.dma_start(out=ov[:, cs, :], in_=C.hT[:, cs, :]),
                reads=[("hT", c) for c in range(4 * q, 4 * q + 4)], dma=("sth", q), is_out=True)


def rstd_from_ps(C, b, n, out_ap, okey, scale, part=128):
    S = C.S
    ps = C.psum
    S.add("act", lambda e: e.activation(out=out_ap, in_=ps[:part, b, :n], func=AF.Ln,
                                        bias=C.epsb[:part, 0:1], scale=scale),
          reads=[("ps", b), ("epsb",)], writes=[okey])
    S.add("act", lambda e: e.activation(out=out_ap, in_=out_ap, func=AF.Exp, scale=-0.5),
          reads=[okey], writes=[okey])


def rmsnorm(C, sqbufs, gcol0, inplace=False):
    S = C.S
    ps = C.psum
    for (t0, n) in TT:
        b = C.nextbank()
        for c in range(NDC):
            si = C.rot("sq", 3)
            S.add("act", lambda e, si=si, c=c, t0=t0, n=n: e.activation(
                out=sqbufs[si][:, :n], in_=C.hT[:, c, t0:t0 + n], func=AF.Square),
                reads=[("hT", c)], writes=[("sq", si)])
            S.add("pe", lambda e, si=si, c=c, b=b, n=n: e.matmul(
                ps[:, b, :n], C.ones[:], sqbufs[si][:, :n], start=(c == 0), stop=(c == NDC - 1)),
                reads=[("sq", si), ("ones",)], writes=[("ps", b)])
        rstd_from_ps(C, b, n, C.rstd[:, t0:t0 + n], ("rstd", t0), 1.0 / D)
    for c in range(NDC):
        for (t0, n) in TT:
            if not inplace:
                S.add("dve", lambda e, c=c, t0=t0, n=n: e.scalar_tensor_tensor(
                    out=C.uT[:, c, t0:t0 + n], in0=C.hT[:, c, t0:t0 + n],
                    scalar=C.prm[:, gcol0 + c:gcol0 + c + 1], in1=C.rstd[:, t0:t0 + n],
                    op0=ALU.mult, op1=ALU.mult),
                    reads=[("hT", c), ("prm",), ("rstd", t0)], writes=[("uT", c, t0)])
            else:
                S.add("dve", lambda e, c=c, t0=t0, n=n: e.scalar_tensor_tensor(
                    out=C.hT[:, c, t0:t0 + n], in0=C.hT[:, c, t0:t0 + n],
                    scalar=C.prm[:, gcol0 + c:gcol0 + c + 1], in1=C.rstd[:, t0:t0 + n],
                    op0=ALU.mult, op1=ALU.mult),
                    reads=[("hT", c), ("prm",), ("rstd", t0)], writes=[("hT", c)])


def st_final_norm(C):
    C.S.barrier()
    A = C.Arena()
    sq = [A.bf16([512]) for _ in range(3)]
    rmsnorm(C, sq, P_FIN, inplace=True)
    C.S.barrier()


def st_ffn(C, wgu_name, wd_name, gcol0):
    S = C.S
    ps = C.psum
    S.barrier()
    A = C.Arena()
    aT = A.bf16([GF, T])
    wgu = [A.bf16([2, NDC, 128]) for _ in range(3)]
    wdb = [A.bf16([GF, 256]) for _ in range(3)]
    sg = [A.bf16([512]) for _ in range(3)]
    sq = [A.bf16([512]) for _ in range(3)]
    if wgu_name not in C.dram:
        C.din(wgu_name, [NFC, 128, 2 * NDC * 128])
        C.din(wd_name, [4, 8, 128, GF * 256])
    wgu_d = C.dram[wgu_name]
    wd_d = C.dram[wd_name]
    rmsnorm(C, sq, C.pb + gcol0)
    for gi, (f0, nf) in enumerate(FGROUPS):
        for fi in range(nf):
            f = f0 + fi
            ws = C.rot("wgu", 3)
            S.add("pool", lambda e, ws=ws, f=f: e.dma_start(
                out=wgu[ws].rearrange("p a c m -> p (a c m)"), in_=wgu_d[f]),
                writes=[("wgu", ws)], dma=("wgu", ws))
            for (t0, n) in TT:
                bg = C.nextbank()
                for c in range(NDC):
                    S.add("pe", lambda e, ws=ws, c=c, bg=bg, t0=t0, n=n: e.matmul(
                        ps[:, bg, :n], wgu[ws][:, 0, c, :], C.uT[:, c, t0:t0 + n],
                        start=(c == 0), stop=(c == NDC - 1)),
                        reads=[("wgu", ws), ("uT", c, t0)], writes=[("ps", bg)])
                bu = C.nextbank()
                for c in range(NDC):
                    S.add("pe", lambda e, ws=ws, c=c, bu=bu, t0=t0, n=n: e.matmul(
                        ps[:, bu, :n], wgu[ws][:, 1, c, :], C.uT[:, c, t0:t0 + n],
                        start=(c == 0), stop=(c == NDC - 1)),
                        reads=[("wgu", ws), ("uT", c, t0)], writes=[("ps", bu)])
                si = C.rot("sg", 3)
                S.add("act", lambda e, si=si, bg=bg, n=n: e.activation(
                    out=sg[si][:, :n], in_=ps[:, bg, :n], func=AF.Silu),
                    reads=[("ps", bg)], writes=[("sg", si)])
                S.add("dve", lambda e, si=si, bu=bu, fi=fi, t0=t0, n=n: e.tensor_tensor(
                    out=aT[:, fi, t0:t0 + n], in0=sg[si][:, :n], in1=ps[:, bu, :n], op=ALU.mult),
                    reads=[("sg", si), ("ps", bu)], writes=[("aT", fi, t0)])
        for dp in range(8):
            ws = C.rot("wd", 3)
            S.add("pool", lambda e, ws=ws, gi=gi, dp=dp: e.dma_start(
                out=wdb[ws].rearrange("p f m -> p (f m)"), in_=wd_d[gi, dp]),
                writes=[("wd", ws)], dma=("wd", ws))
            for dd in range(2):
                dc = 2 * dp + dd
                for (t0, n) in TT:
                    b = C.nextbank()
                    for fi in range(nf):
                        S.add("pe", lambda e, ws=ws, fi=fi, dd=dd, b=b, t0=t0, n=n: e.matmul(
                            ps[:, b, :n], wdb[ws][:, fi, dd * 128:(dd + 1) * 128], aT[:, fi, t0:t0 + n],
                            start=(fi == 0), stop=(fi == nf - 1)),
                            reads=[("wd", ws), ("aT", fi, t0)], writes=[("ps", b)])
                    S.add("dve", lambda e, b=b, dc=dc, t0=t0, n=n: e.scalar_tensor_tensor(
                        out=C.hT[:, dc, t0:t0 + n], in0=ps[:, b, :n], scalar=0.5,
                        in1=C.hT[:, dc, t0:t0 + n], op0=ALU.mult, op1=ALU.add),
                        reads=[("ps", b), ("hT", dc)], writes=[("hT", dc)])
    S.barrier()


def load_block(C, wsl, src, RC, cols, cap=4096):
    cols = np.asarray(cols)
    n = RC * len(cols)
    assert n <= cap
    off = C.blk_off
    C.blk_off += n
    C.blocks.append(((src, C.L), RC, cols))
    ws = C.rot("wsl", 3)
    view = wsl[ws][:, 0:n].rearrange("p (c m) -> p c m", c=RC)
    C.S.add("pool", lambda e: e.dma_start(out=wsl[ws][:, 0:n], in_=C.dram["wblk"][:, off:off + n]),
            writes=[("wsl", ws)], dma=("wsl", ws))
    return ws, view


def proj_fm(C, ws, view, c0, m, t0, n):
    b = C.nextbank()
    uk = ukey(t0)
    for c in range(NDC):
        C.S.add("pe", lambda e, c=c: e.matmul(
            C.psum[0:m, b, 0:n], view[:, c, c0:c0 + m], C.uT[:, c, t0:t0 + n],
            start=(c == 0), stop=(c == NDC - 1)),
            reads=[("wsl", ws), ("uT", c, uk)], writes=[("ps", b)])
    return b


def proj_tm(C, ws, view, ncols, t0, n):
    b = C.nextbank()
    uk = ukey(t0)
    for c in range(NDC):
        C.S.add("pe", lambda e, c=c: e.matmul(
            C.psum[0:n, b, 0:ncols], C.uT[:, c, t0:t0 + n], view[:, c, 0:ncols],
            start=(c == 0), stop=(c == NDC - 1)),
            reads=[("wsl", ws), ("uT", c, uk)], writes=[("ps", b)])
    return b


def st_set_layer(C, L):
    C.L = L
    C.pb = 256 * L


def st_mixer(C, mode):
    S = C.S
    ps = C.psum
    full = (mode == "full")
    L = C.L
    S.barrier()
    A = C.Arena()
    sq = [A.bf16([512]) for _ in range(3)]
    if not full:
        rmsnorm(C, sq, C.pb + P_MIX)
    wsl = [A.bf16([4096]) for _ in range(3)]
    tbl = A.f32([4400])
    C.glr1T = A.f32([T])
    C.w2b = A.f32([512])
    B = Ctx()
    B.ktm, B.gt, B.e1, B.e2, B.eE = [A.f32([256]) for _ in range(5)]
    B.khat = [A.bf16([256]) for _ in range(2)]
    B.vt = [A.bf16([256]) for _ in range(2)]
    B.decs = [A.f32([2, 4]) for _ in range(2)]
    B.Sst = A.f32([2, 256])
    B.dtot = A.f32([2])
    B.lbc = A.f32([40])
    B.xst = [A.f32([516]) for _ in range(2)]
    B.dpr = A.f32([2])
    if full:
        B.qraw, B.kraw, B.sgT = A.bf16([2, T]), A.bf16([2, T]), A.bf16([2, T])
        B.eq, B.ek = A.f32([2, 128]), A.f32([2, 128])
        B.qt = [A.bf16([2, 128]) for _ in range(2)]
        B.kt = [A.bf16([2, 128]) for _ in range(2)]
        B.Pm = [A.bf16([128]) for _ in range(2)]
        B.osb = [A.f32([2, 128]) for _ in range(2)]
        B.sqo = [A.bf16([2, 128]) for _ in range(2)]
        B.rs = [A.f32([128]) for _ in range(2)]
        B.yst = [A.bf16([2, 128]) for _ in range(3)]
        B.Sb = A.bf16([4, 256])
        B.Sin = A.f32([2, 256])
        B.Sld = A.f32([2, 256])
        B.dsl = A.f32([2])
        B.fA, B.fB, B.fC = A.f32([512]), A.f32([512]), A.f32([512])
        if "ysd" not in C.dram:
            C.dint("ysd", [24, 128, T], BF16)
    w2b_d = C.din(f"w2b{L}", [17, 512]) if f"w2b{L}" not in C.dram else C.dram[f"w2b{L}"]
    S.add("sp", lambda e: e.dma_start(out=C.w2b[0:17, :], in_=w2b_d), writes=[("w2b",)], dma=("w2b",))
    S.add("dve", lambda e: e.memset(C.glr1T[:], 1.0), writes=[("glr",)])
    ws, view = load_block(C, wsl, "w_in", NDC, np.arange(GLR, GLR + 16))
    for (t0, n) in TT:
        b = proj_fm(C, ws, view, 0, 16, t0, n)
        S.add("act", lambda e, b=b, t0=t0, n=n: e.activation(
            out=C.glr1T[0:16, t0:t0 + n], in_=ps[0:16, b, 0:n], func=AF.Copy),
            reads=[("ps", b)], writes=[("glr",)])
    lbc = B.lbc
    s0c, s1c, lbcol, omlc, nomlc = (lbc[:, 0:8], lbc[:, 8:16], lbc[:, 16:24], lbc[:, 24:32], lbc[:, 32:40])
    l0c, l1c = C.prm[:, C.pb + P_LB0:C.pb + P_LB0 + 8], C.prm[:, C.pb + P_LB1:C.pb + P_LB1 + 8]
    flagc = C.prm[:, C.pb + 104:C.pb + 105]

    def lb_compute(l0, l1, s0, s1, lb, oml, noml, flag_ap, rk, wk):
        S.add("dve", lambda e: e.tensor_tensor(out=s1, in0=l0, in1=l1, op=ALU.subtract), reads=rk + wk, writes=wk)
        S.add("dve", lambda e: e.tensor_scalar(out=s0, in0=s1, scalar1=-1.0, scalar2=None, op0=ALU.mult), reads=wk, writes=wk)
        S.add("act", lambda e: e.activation(out=s0, in_=s0, func=AF.Exp), reads=wk, writes=wk)
        S.add("act", lambda e: e.activation(out=s1, in_=s1, func=AF.Exp), reads=wk, writes=wk)
        S.add("dve", lambda e: e.tensor_scalar(out=s0, in0=s0, scalar1=1.0, scalar2=None, op0=ALU.add), reads=wk, writes=wk)
        S.add("dve", lambda e: e.tensor_scalar(out=s1, in0=s1, scalar1=1.0, scalar2=None, op0=ALU.add), reads=wk, writes=wk)
        S.add("dve", lambda e: e.reciprocal(out=s0, in_=s0), reads=wk, writes=wk)
        S.add("dve", lambda e: e.reciprocal(out=s1, in_=s1), reads=wk, writes=wk)
        S.add("dve", lambda e: e.scalar_tensor_tensor(out=lb, in0=s1, scalar=flag_ap, in1=s0, op0=ALU.mult, op1=ALU.add),
              reads=wk + rk, writes=wk)
        S.add("dve", lambda e: e.tensor_tensor(out=lb, in0=lb, in1=s0, op=ALU.subtract), reads=wk, writes=wk)
        S.add("dve", lambda e: e.tensor_scalar(out=oml, in0=lb, scalar1=-1.0, scalar2=1.0, op0=ALU.mult, op1=ALU.add),
              reads=wk, writes=wk)
        if noml is not None:
            S.add("dve", lambda e: e.tensor_scalar(out=noml, in0=oml, scalar1=-1.0, scalar2=None, op0=ALU.mult),
                  reads=wk, writes=wk)
    lb_compute(l0c, l1c, s0c, s1c, lbcol, omlc, nomlc, flagc, [("prm",)], [("lbc",)])

    for br in ("gla", "ret", "hg"):
        p = BR[br]
        KC, VC = p["KC"], p["VC"]
        DK, DV = KC * 128, VC * 128
        nh = p["nh"]
        XW = KC * DV + KC
        if not full:
            xs_d = C.dint(f"xs_{L}_{br}", [3 * nh * 128, XW])
            xd_d = C.dint(f"xd_{L}_{br}", [3 * nh * 128, XW])
        else:
            xd_d = C.dram[f"xd_{L}_{br}"]
        if br == "ret":
            rotF = C.din("rotF", [128, 2 * T]) if "rotF" not in C.dram else C.dram["rotF"]
            rotT = C.din("rotT", [128, 2 * 9 * 128]) if "rotT" not in C.dram else C.dram["rotT"]
            S.add("sp", lambda e: e.dma_start(out=tbl[:, 0:2 * T], in_=rotF), writes=[("tbl",)], dma=("tbl", 0))
            S.add("sp", lambda e: e.dma_start(out=tbl[:, 2 * T:2 * T + 2304], in_=rotT), writes=[("tbl",)], dma=("tbl", 1))
            cosF, sinF = tbl[:, 0:T], tbl[:, T:2 * T]
            cosT = tbl[:, 2 * T:2 * T + 1152].rearrange("p (a b) -> p a b", a=9)
            sinT = tbl[:, 2 * T + 1152:2 * T + 2304].rearrange("p (a b) -> p a b", a=9)
        if br == "hg":
            lbl = C.din("lbl", [128, 2048]) if "lbl" not in C.dram else C.dram["lbl"]
            S.add("sp", lambda e: e.dma_start(out=tbl[:, 2048:4096], in_=lbl), writes=[("tbl",)], dma=("tbl", 0))
            lbb, omlb = tbl[:, 0:1024], tbl[:, 1024:2048]
            lb_compute(tbl[:, 2048:3072], tbl[:, 3072:4096], tbl[:, 2048:3072], tbl[:, 3072:4096],
                       lbb, omlb, None, flagc[:, 0:1], [("prm",)], [("tbl",)])
        for h in range(nh):
            mix_head(C, B, wsl, br, h, full, locals())
    if not full:
        rg = [[0, 1, 2, 3], [4, 5, 6, 7]]
        for br in ("gla", "ret", "hg"):
            nh = BR[br]["nh"]
            xs_d, xd_d = C.dram[f"xs_{L}_{br}"], C.dram[f"xd_{L}_{br}"]
            S.add("pool", lambda e, xs_d=xs_d, xd_d=xd_d: e.collective_compute(
                "AllReduce", ALU.add, replica_groups=rg, ins=[xs_d.opt()], outs=[xd_d.opt()]),
                reads=[("xs", L, br, h, s) for h in range(nh) for s in range(3)], writes=[("xd", L, br)],
                dma=("cc", L, br), inc=1)
    S.barrier()


def mix_head(C, B, wsl, br, h, full, env):
    S = C.S
    ps = C.psum
    p = BR[br]
    KC, VC = p["KC"], p["VC"]
    DK, DV = KC * 128, VC * 128
    qs = p["qs"]
    tbl = env["tbl"]
    kcols = np.arange(p["k"] + h * DK, p["k"] + (h + 1) * DK)
    vcols = np.arange(p["v"] + h * DV, p["v"] + (h + 1) * DV)
    qcols = np.arange(p["q"] + h * DK, p["q"] + (h + 1) * DK)
    gcols = np.arange(p["g"] + h * DV, p["g"] + (h + 1) * DV)
    if br == "ret":
        cosF, sinF, cosT, sinT = env["cosF"], env["sinF"], env["cosT"], env["sinT"]
    if br == "hg":
        lbb, omlb = env["lbb"], env["omlb"]
        omlc, nomlc = env["omlc"], env["nomlc"]
    RT = [("tbl",)]

    def dve(fn, reads, writes):
        S.add("dve", fn, reads=reads, writes=writes)

    def act(fn, reads, writes):
        S.add("act", fn, reads=reads, writes=writes)

    def fm_tile(role, nch, ws, view, t0, n):
        if True:
            if True:
                bs_ = [proj_fm(C, ws, view, ch * 128, 128, t0, n) for ch in range(nch)]
                dst = {"q": B.qraw, "k": B.kraw, "g": B.sgT}[role]
                dkey = {"q": "qraw", "k": "kraw", "g": "sgT"}[role]
                if role == "g":
                    for ch, b in enumerate(bs_):
                        act(lambda e, b=b, ch=ch: e.activation(out=dst[:, ch, t0:t0 + n], in_=ps[:, b, 0:n], func=AF.Silu),
                            [("ps", b)], [(dkey, ch, t0)])
                elif role == "k" and br == "hg":
                    for ch, b in enumerate(bs_):
                        act(lambda e, b=b, n=n: e.activation(out=B.fA[:, 0:n], in_=ps[:, b, 0:n], func=AF.Exp, scale=-1.0),
                            [("ps", b)], [("fA",)])
                        dve(lambda e, n=n: e.tensor_scalar(out=B.fA[:, 0:n], in0=B.fA[:, 0:n], scalar1=1.0, scalar2=None, op0=ALU.add),
                            [("fA",)], [("fA",)])
                        dve(lambda e, n=n: e.reciprocal(out=B.fA[:, 0:n], in_=B.fA[:, 0:n]), [("fA",)], [("fA",)])
                        if role == "g":
                            dve(lambda e, b=b, ch=ch, t0=t0, n=n: e.tensor_tensor(
                                out=dst[:, ch, t0:t0 + n], in0=B.fA[:, 0:n], in1=ps[:, b, 0:n], op=ALU.mult),
                                [("fA",), ("ps", b)], [(dkey, ch, t0)])
                        else:
                            dve(lambda e, ch=ch, t0=t0, n=n: e.tensor_scalar(
                                out=dst[:, ch, t0:t0 + n], in0=B.fA[:, 0:n], scalar1=nomlc[:, h:h + 1],
                                scalar2=omlc[:, h:h + 1], op0=ALU.mult, op1=ALU.add),
                                [("fA",), ("lbc",)], [(dkey, ch, t0)])
                elif br == "ret":
                    b0, b1 = bs_
                    sc = qs if role == "q" else 1.0
                    dve(lambda e, n=n, t0=t0: e.scalar_tensor_tensor(out=B.fA[:, 0:n], in0=ps[:, b0, 0:n], scalar=sc,
                        in1=cosF[:, t0:t0 + n], op0=ALU.mult, op1=ALU.mult), [("ps", b0)] + RT, [("fA",)])
                    dve(lambda e, n=n, t0=t0: e.scalar_tensor_tensor(out=B.fB[:, 0:n], in0=ps[:, b1, 0:n], scalar=sc,
                        in1=sinF[:, t0:t0 + n], op0=ALU.mult, op1=ALU.mult), [("ps", b1)] + RT, [("fB",)])
                    dve(lambda e, n=n, t0=t0: e.tensor_tensor(out=dst[:, 0, t0:t0 + n], in0=B.fA[:, 0:n], in1=B.fB[:, 0:n],
                        op=ALU.subtract), [("fA",), ("fB",)], [(dkey, 0, t0)])
                    dve(lambda e, n=n, t0=t0: e.scalar_tensor_tensor(out=B.fA[:, 0:n], in0=ps[:, b0, 0:n], scalar=sc,
                        in1=sinF[:, t0:t0 + n], op0=ALU.mult, op1=ALU.mult), [("ps", b0)] + RT, [("fA",)])
                    dve(lambda e, n=n, t0=t0: e.scalar_tensor_tensor(out=B.fB[:, 0:n], in0=ps[:, b1, 0:n], scalar=sc,
                        in1=cosF[:, t0:t0 + n], op0=ALU.mult, op1=ALU.mult), [("ps", b1)] + RT, [("fB",)])
                    dve(lambda e, n=n, t0=t0: e.tensor_tensor(out=dst[:, 1, t0:t0 + n], in0=B.fA[:, 0:n], in1=B.fB[:, 0:n],
                        op=ALU.add), [("fA",), ("fB",)], [(dkey, 1, t0)])
                else:
                    sc = qs if role == "q" else 1.0
                    b = bs_[0]
                    act(lambda e, b=b, t0=t0, n=n: e.activation(out=dst[:, 0, t0:t0 + n], in_=ps[:, b, 0:n],
                        func=AF.Copy, scale=sc), [("ps", b)], [(dkey, 0, t0)])

    if full:
        for role, cols, nch in (("q", qcols, KC), ("k", kcols, KC), ("g", gcols, VC)):
            ws, view = load_block(C, wsl, "w_in", NDC, cols)
            for (t0, n) in TT:
                fm_tile(role, nch, ws, view, t0, n)

    dve(lambda e: e.memset(B.Sst[:, 0:KC, 0:DV], 0.0), [], [("Sst",)])
    if not full:
        dve(lambda e: e.memset(B.dtot[:, 0:KC], 1.0), [], [("dtot",)])
    else:
        xd_d = env["xd_d"]
        nh_ = p["nh"]
        XW = KC * DV + KC
        dve(lambda e: e.memset(B.Sin[:, 0:KC, 0:DV], 0.0), [], [("Sin",)])
        for s in range(3):
            r0_ = (s * nh_ + h) * 128
            xi = C.rot("xst", 2)
            xst = B.xst[xi]
            XK = ("xst", xi)
            S.add("sp", lambda e, r0_=r0_, xst=xst: e.dma_start(out=xst[:, 0:XW], in_=xd_d[r0_:r0_ + 128, :]),
                  reads=[("xd", env["L"], br)], writes=[XK], dma=("xsw", xi))
            wcol = C.cm[:, 4 + s:5 + s]
            dve(lambda e, xst=xst: e.tensor_scalar(out=B.dpr[:, 0:KC], in0=xst[:, KC * DV:XW], scalar1=-1.0, scalar2=None,
                                                   op0=ALU.add), [XK], [("dpr",)])
            dve(lambda e, wcol=wcol: e.tensor_scalar(out=B.dpr[:, 0:KC], in0=B.dpr[:, 0:KC], scalar1=wcol, scalar2=None,
                                                     op0=ALU.mult), [("dpr",), ("cm",)], [("dpr",)])
            dve(lambda e: e.tensor_scalar(out=B.dpr[:, 0:KC], in0=B.dpr[:, 0:KC], scalar1=1.0, scalar2=None, op0=ALU.add),
                [("dpr",)], [("dpr",)])
            dve(lambda e, xst=xst, wcol=wcol: e.tensor_scalar(out=xst[:, 0:KC * DV], in0=xst[:, 0:KC * DV], scalar1=wcol,
                                                              scalar2=None, op0=ALU.mult), [XK, ("cm",)], [XK])
            for kc in range(KC):
                dve(lambda e, kc=kc, xst=xst: e.scalar_tensor_tensor(out=B.Sin[:, kc, 0:DV], in0=B.Sin[:, kc, 0:DV],
                    scalar=B.dpr[:, kc:kc + 1], in1=xst[:, kc * DV:(kc + 1) * DV], op0=ALU.mult, op1=ALU.add),
                    [("Sin",), ("dpr",), XK], [("Sin",)])
    if br == "ret":
        dve(lambda e: e.memset(B.gt[:, 0:DK], RET_LOGG[h]), [], [("gt",)])

    wsk, vk = load_block(C, wsl, "w_in", NDC, kcols)
    wsv, vv = load_block(C, wsl, "w_in", NDC, vcols)
    hc = slice(h * 128, (h + 1) * 128)

    def tile(ti, s0, n):
        di = C.rot("decs", 2)
        decs = B.decs[di]
        DKY = ("decs", di)
        if full:
            oi = C.rot("osb", 2)
            osb, sqo, rs = B.osb[oi], B.sqo[oi], B.rs[oi]
        ki = C.rot("khat", 2)
        vi = C.rot("vt", 2)
        khat, vt = B.khat[ki], B.vt[vi]
        KH, VT = ("khat", ki), ("vt", vi)
        bk = proj_tm(C, wsk, vk, DK, s0, n)
        bv = proj_tm(C, wsv, vv, DV, s0, n)
        if br == "gla":
            act(lambda e: e.activation(out=vt[0:n, 0:DV], in_=ps[0:n, bv, 0:DV], func=AF.Copy), [("ps", bv)], [VT])
            act(lambda e: e.activation(out=B.ktm[0:n, 0:DK], in_=ps[0:n, bk, 0:DK], func=AF.Copy), [("ps", bk)], [("ktm",)])
            bz = C.nextbank()
            S.add("pe", lambda e: e.matmul(ps[0:n, bz, 0:128], C.glr1T[0:17, s0:s0 + n], C.w2b[0:17, h * 128:(h + 1) * 128],
                                           start=True, stop=True), reads=[("glr",), ("w2b",)], writes=[("ps", bz)])
            act(lambda e: e.activation(out=B.e1[0:n, 0:128], in_=ps[0:n, bz, 0:128], func=AF.Exp, scale=-1.0),
                [("ps", bz)], [("e1",)])
            act(lambda e: e.activation(out=B.gt[0:n, 0:128], in_=B.e1[0:n, 0:128], func=AF.Ln, bias=C.onesf[0:n, 0:1], scale=1.0),
                [("e1",), ("onesf",)], [("gt",)])
            dve(lambda e: e.tensor_scalar(out=B.gt[0:n, 0:128], in0=B.gt[0:n, 0:128], scalar1=-1.0 / 16.0, scalar2=None,
                                          op0=ALU.mult), [("gt",)], [("gt",)])
        elif br == "ret":
            act(lambda e: e.activation(out=vt[0:n, 0:DV], in_=ps[0:n, bv, 0:DV], func=AF.Copy), [("ps", bv)], [VT])
            for half, (opA, tA, tB) in enumerate(((ALU.subtract, cosT, sinT), (ALU.add, sinT, cosT))):
                dve(lambda e, tA=tA: e.tensor_tensor(out=B.e1[0:n, 0:128], in0=ps[0:n, bk, 0:128], in1=tA[0:n, ti, :],
                                                    op=ALU.mult), [("ps", bk)] + RT, [("e1",)])
                dve(lambda e, tB=tB: e.tensor_tensor(out=B.e2[0:n, 0:128], in0=ps[0:n, bk, 128:256], in1=tB[0:n, ti, :],
                                                    op=ALU.mult), [("ps", bk)] + RT, [("e2",)])
                dve(lambda e, half=half, opA=opA: e.tensor_tensor(out=B.ktm[0:n, half * 128:(half + 1) * 128],
                                                                  in0=B.e1[0:n, 0:128], in1=B.e2[0:n, 0:128], op=opA),
                    [("e1",), ("e2",)], [("ktm",)])
        else:
            act(lambda e: e.activation(out=B.e1[0:n, 0:128], in_=ps[0:n, bk, 0:128], func=AF.Exp, scale=-1.0),
                [("ps", bk)], [("e1",)])
            dve(lambda e: e.tensor_scalar(out=B.e1[0:n, 0:128], in0=B.e1[0:n, 0:128], scalar1=1.0, scalar2=None, op0=ALU.add),
                [("e1",)], [("e1",)])
            dve(lambda e: e.reciprocal(out=B.e1[0:n, 0:128], in_=B.e1[0:n, 0:128]), [("e1",)], [("e1",)])
            dve(lambda e: e.tensor_tensor(out=B.e2[0:n, 0:128], in0=B.e1[0:n, 0:128], in1=omlb[0:n, hc], op=ALU.mult),
                [("e1",)] + RT, [("e2",)])
            dve(lambda e: e.scalar_tensor_tensor(out=B.gt[0:n, 0:128], in0=B.e2[0:n, 0:128], scalar=1e-20, in1=lbb[0:n, hc],
                                                 op0=ALU.max, op1=ALU.add), [("e2",)] + RT, [("gt",)])
            act(lambda e: e.activation(out=B.gt[0:n, 0:128], in_=B.gt[0:n, 0:128], func=AF.Ln), [("gt",)], [("gt",)])
            dve(lambda e: e.tensor_tensor(out=B.ktm[0:n, 0:128], in0=omlb[0:n, hc], in1=B.e2[0:n, 0:128], op=ALU.subtract),
                [("e2",)] + RT, [("ktm",)])
            act(lambda e: e.activation(out=B.eE[0:n, 0:128], in_=ps[0:n, bv, 0:128], func=AF.Exp, scale=-1.0),
                [("ps", bv)], [("eE",)])
            dve(lambda e: e.tensor_scalar(out=B.eE[0:n, 0:128], in0=B.eE[0:n, 0:128], scalar1=1.0, scalar2=None, op0=ALU.add),
                [("eE",)], [("eE",)])
            dve(lambda e: e.reciprocal(out=B.eE[0:n, 0:128], in_=B.eE[0:n, 0:128]), [("eE",)], [("eE",)])
            dve(lambda e: e.tensor_tensor(out=vt[0:n, 0:128], in0=B.eE[0:n, 0:128], in1=ps[0:n, bv, 0:128], op=ALU.mult),
                [("eE",), ("ps", bv)], [VT])
        bE = C.nextbank()
        S.add("pe", lambda e: e.matmul(ps[0:n, bE, 0:DK], C.cU[0:n, 0:n], B.gt[0:n, 0:DK], start=True, stop=True),
              reads=[("cU",), ("gt",)], writes=[("ps", bE)])
        act(lambda e: e.activation(out=B.eE[0:n, 0:DK], in_=ps[0:n, bE, 0:DK], func=AF.Exp), [("ps", bE)], [("eE",)])
        dve(lambda e: e.tensor_tensor(out=khat[0:n, 0:DK], in0=B.ktm[0:n, 0:DK], in1=B.eE[0:n, 0:DK], op=ALU.mult),
            [("ktm",), ("eE",)], [KH])
        Mq = C.cMq16 if n == 16 else C.cMq
        mqk = ("cMq16",) if n == 16 else ("cMq",)
        for kc in range(KC):
            bX = C.nextbank()
            S.add("pe", lambda e, kc=kc, bX=bX: e.matmul(ps[:, bX, 0:n + 4], B.gt[0:n, kc * 128:(kc + 1) * 128],
                                                         Mq[0:n, 0:n + 4], start=True, stop=True),
                  reads=[("gt",), mqk], writes=[("ps", bX)])
            act(lambda e, kc=kc, bX=bX: e.activation(out=decs[:, kc, 0:4], in_=ps[:, bX, n:n + 4], func=AF.Exp),
                [("ps", bX)], [DKY])
            if full:
                act(lambda e, kc=kc, bX=bX: e.activation(out=B.eq[:, kc, 0:n], in_=ps[:, bX, 0:n], func=AF.Exp),
                    [("ps", bX)], [("eq",)])
                act(lambda e, kc=kc, bX=bX: e.activation(out=B.ek[:, kc, 0:n], in_=ps[:, bX, 0:n], func=AF.Exp, scale=-1.0),
                    [("ps", bX)], [("ek",)])
        if full:
            qi = C.rot("qt", 2)
            pi = C.rot("Pm", 2)
            qt, kt, Pm = B.qt[qi], B.kt[qi], B.Pm[pi]
            tk = ukey(s0)
            dve(lambda e: e.tensor_tensor(out=qt[:, 0:KC, 0:n], in0=B.qraw[:, 0:KC, s0:s0 + n], in1=B.eq[:, 0:KC, 0:n], op=ALU.mult),
                [("qraw", kc, tk) for kc in range(KC)] + [("eq",)], [("qt", qi)])
            dve(lambda e: e.tensor_tensor(out=kt[:, 0:KC, 0:n], in0=B.kraw[:, 0:KC, s0:s0 + n], in1=B.ek[:, 0:KC, 0:n], op=ALU.mult),
                [("kraw", kc, tk) for kc in range(KC)] + [("ek",)], [("kt", qi)])
            bs = C.nextbank()
            for kc in range(KC):
                S.add("pe", lambda e, kc=kc: e.matmul(ps[0:n, bs, 0:n], kt[:, kc, 0:n], qt[:, kc, 0:n],
                                                     start=(kc == 0), stop=(kc == KC - 1)),
                      reads=[("kt", qi), ("qt", qi)], writes=[("ps", bs)])
            dve(lambda e: e.tensor_tensor(out=Pm[0:n, 0:n], in0=ps[0:n, bs, 0:n], in1=C.cmask[0:n, 0:n], op=ALU.mult),
                [("ps", bs), ("cmask",)], [("Pm", pi)])
        yield "front"
        chunks = [(0, min(n, 64))] + ([(64, 64)] if n == 128 else [])
        for ci, (r0, rn) in enumerate(chunks):
            if full:
                for kc in range(KC):
                    dve(lambda e, kc=kc, ci=ci: e.tensor_scalar(out=B.Sb[:, ci * 2 + kc, 0:DV], in0=B.Sst[:, kc, 0:DV],
                        scalar1=decs[:, kc, 2 + ci:3 + ci], scalar2=None, op0=ALU.mult),
                        [("Sst",), DKY], [("Sb", ci)])
            for kc in range(KC):
                bS = C.nextbank()
                S.add("pe", lambda e, kc=kc, bS=bS, r0=r0, rn=rn: e.matmul(
                    ps[:, bS, 0:DV], khat[r0:r0 + rn, kc * 128:(kc + 1) * 128], vt[r0:r0 + rn, 0:DV], start=True, stop=True),
                    reads=[KH, VT], writes=[("ps", bS)])
                dve(lambda e, kc=kc, bS=bS, ci=ci: e.scalar_tensor_tensor(
                    out=B.Sst[:, kc, 0:DV], in0=B.Sst[:, kc, 0:DV], scalar=decs[:, kc, ci:ci + 1], in1=ps[:, bS, 0:DV],
                    op0=ALU.mult, op1=ALU.add), [("Sst",), DKY, ("ps", bS)], [("Sst",)])
                if (not full) and ti >= 1:
                    dve(lambda e, kc=kc, ci=ci: e.tensor_tensor(out=B.dtot[:, kc:kc + 1], in0=B.dtot[:, kc:kc + 1],
                        in1=decs[:, kc, ci:ci + 1], op=ALU.mult), [("dtot",), DKY], [("dtot",)])
        if full:
            for vc in range(VC):
                bo = C.nextbank()
                S.add("pe", lambda e, vc=vc, bo=bo: e.matmul(ps[:, bo, 0:n], vt[0:n, vc * 128:(vc + 1) * 128], Pm[0:n, 0:n],
                                                           start=True, stop=False),
                      reads=[VT, ("Pm", pi)], writes=[("ps", bo)])
                last = (len(chunks) - 1, KC - 1)
                for ci, (r0, rn) in enumerate(chunks):
                    for kc in range(KC):
                        S.add("pe", lambda e, vc=vc, bo=bo, ci=ci, kc=kc, r0=r0, rn=rn: e.matmul(
                            ps[:, bo, r0:r0 + rn], B.Sb[:, ci * 2 + kc, vc * 128:(vc + 1) * 128], qt[:, kc, r0:r0 + rn],
                            start=False, stop=((ci, kc) == last)),
                            reads=[("Sb", ci), ("qt", qi)], writes=[("ps", bo)])
                act(lambda e, vc=vc, bo=bo: e.activation(out=osb[:, vc, 0:n], in_=ps[:, bo, 0:n], func=AF.Copy),
                    [("ps", bo)], [("osb", oi, vc)])
            if ti == 0:
                dve(lambda e: e.tensor_tensor(out=B.Sst[:, 0:KC, 0:DV], in0=B.Sst[:, 0:KC, 0:DV], in1=B.Sin[:, 0:KC, 0:DV],
                                              op=ALU.add), [("Sst",), ("Sin",)], [("Sst",)])
        yield "mid"
        if full:
            if br == "ret":
                bm = C.nextbank()
                for vc in range(VC):
                    S.add("pe", lambda e, vc=vc: e.matmul(ps[:, bm, 0:n], C.onesf[:], osb[:, vc, 0:n],
                                                         start=(vc == 0), stop=(vc == VC - 1)),
                          reads=[("onesf",), ("osb", oi, vc)], writes=[("ps", bm)])
                for vc in range(VC):
                    dve(lambda e, vc=vc: e.scalar_tensor_tensor(out=osb[:, vc, 0:n], in0=ps[:, bm, 0:n], scalar=-1.0 / DV,
                        in1=osb[:, vc, 0:n], op0=ALU.mult, op1=ALU.add), [("ps", bm), ("osb", oi, vc)], [("osb", oi, vc)])
            bq = C.nextbank()
            for vc in range(VC):
                act(lambda e, vc=vc: e.activation(out=sqo[:, vc, 0:n], in_=osb[:, vc, 0:n], func=AF.Square),
                    [("osb", oi, vc)], [("sqo", oi, vc)])
                S.add("pe", lambda e, vc=vc: e.matmul(ps[:, bq, 0:n], C.ones[:], sqo[:, vc, 0:n],
                                                     start=(vc == 0), stop=(vc == VC - 1)),
                      reads=[("ones",), ("sqo", oi, vc)], writes=[("ps", bq)])
            rstd_from_ps(C, bq, n, rs[:, 0:n], ("rs", oi), 1.0 / DV)
            yi = C.rot("yst", 3)
            yst = B.yst[yi]
            for vc in range(VC):
                gcol = C.pb + p["ncol"] + h * VC + vc
                dve(lambda e, vc=vc, gcol=gcol: e.scalar_tensor_tensor(out=osb[:, vc, 0:n], in0=osb[:, vc, 0:n],
                    scalar=C.prm[:, gcol:gcol + 1], in1=rs[:, 0:n], op0=ALU.mult, op1=ALU.mult),
                    [("osb", oi, vc), ("prm",), ("rs", oi)], [("osb", oi, vc)])
                dve(lambda e, vc=vc: e.tensor_tensor(out=yst[:, vc, 0:n], in0=osb[:, vc, 0:n], in1=B.sgT[:, vc, s0:s0 + n],
                    op=ALU.mult), [("osb", oi, vc), ("sgT", vc, ukey(s0))], [("yst", yi)])
            ch0 = p["ys0"] + h * VC
            ysd = C.dram["ysd"]
            S.add("sp", lambda e: e.dma_start(out=ysd[ch0:ch0 + VC, :, s0:s0 + n].rearrange("c p t -> p c t"),
                                              in_=yst[:, 0:VC, 0:n]),
                  reads=[("yst", yi)], writes=[("ysd", ch0 + vc, ti) for vc in range(VC)], dma=("ysw", yi))
        yield "back"

    gens = [tile(ti, s0, n) for ti, (s0, n) in enumerate(TM)]
    NT = len(TM)
    next(gens[0])
    next(gens[1])
    next(gens[0])
    for t in range(1, NT):
        if t + 1 < NT:
            next(gens[t + 1])
        next(gens[t - 1])
        next(gens[t])
    next(gens[NT - 1])
    if not full:
        xs_d = env["xs_d"]
        nh_ = p["nh"]
        XW = KC * DV + KC
        for s in range(3):
            r0_ = (s * nh_ + h) * 128
            xi = C.rot("xst", 2)
            xst = B.xst[xi]
            XK = ("xst", xi)
            mcol = C.cm[:, s:s + 1]
            dve(lambda e, xst=xst, mcol=mcol: e.tensor_scalar(
                out=xst[:, 0:KC * DV].rearrange("p (k v) -> p k v", k=KC), in0=B.Sst[:, 0:KC, 0:DV], scalar1=mcol,
                scalar2=None, op0=ALU.mult), [("Sst",), ("cm",)], [XK])
            dve(lambda e, xst=xst, mcol=mcol: e.tensor_scalar(out=xst[:, KC * DV:XW], in0=B.dtot[:, 0:KC], scalar1=mcol,
                                                              scalar2=None, op0=ALU.mult), [("dtot",), ("cm",), XK], [XK])
            S.add("sp", lambda e, r0_=r0_, xst=xst: e.dma_start(out=xs_d[r0_:r0_ + 128, :], in_=xst[:, 0:XW]),
                  reads=[XK], writes=[("xs", env["L"], br, h, s)], dma=("xsw", xi))

def st_merge(C):
    S = C.S
    ps = C.psum
    S.barrier()
    A = C.Arena()
    ysT = A.bf16([24, T])
    mergedT = A.bf16([NDC, T])
    wsl = [A.bf16([2048]) for _ in range(3)]
    fS = [A.f32([512]) for _ in range(2)]
    acc = C.rstd
    ysd = C.dram["ysd"]
    for q in range(3):
        S.add("sp", lambda e, q=q: e.dma_start(out=ysT[:, 8 * q:8 * q + 8, :],
                                               in_=ysd[8 * q:8 * q + 8].rearrange("c p t -> p c t")),
              reads=[("ysd", c, ti) for c in range(8 * q, 8 * q + 8) for ti in range(9)],
              writes=[("ysT", q)], dma=("ysr", q))

    def merge_tile(dc, nb, wsb, wbv, wsm, mgv, t0, n):
        bz = C.nextbank()
        for wc in range(8):
            S.add("pe", lambda e, wc=wc: e.matmul(ps[:, bz, 0:n], wbv[:, wc, :], ysT[:, nb * 8 + wc, t0:t0 + n],
                                                 start=(wc == 0), stop=(wc == 7)),
                  reads=[("wsl", wsb), ("ysT", nb)], writes=[("ps", bz)])
        bg = proj_fm(C, wsm, mgv, 0, 128, t0, n)
        fi = C.rot("fS", 2)
        f = fS[fi]
        S.add("act", lambda e: e.activation(out=f[:, 0:n], in_=ps[:, bg, 0:n], func=AF.Sigmoid),
              reads=[("ps", bg)], writes=[("fS", fi)])
        if nb == 0:
            S.add("dve", lambda e: e.tensor_tensor(out=acc[:, t0:t0 + n], in0=f[:, 0:n], in1=ps[:, bz, 0:n], op=ALU.mult),
                  reads=[("fS", fi), ("ps", bz)], writes=[("acc", t0)])
        else:
            S.add("dve", lambda e: e.tensor_tensor(out=f[:, 0:n], in0=f[:, 0:n], in1=ps[:, bz, 0:n], op=ALU.mult),
                  reads=[("fS", fi), ("ps", bz)], writes=[("fS", fi)])
            if nb == 1:
                S.add("dve", lambda e: e.tensor_tensor(out=acc[:, t0:t0 + n], in0=acc[:, t0:t0 + n], in1=f[:, 0:n], op=ALU.add),
                      reads=[("fS", fi), ("acc", t0)], writes=[("acc", t0)])
            else:
                S.add("dve", lambda e: e.tensor_tensor(out=mergedT[:, dc, t0:t0 + n], in0=acc[:, t0:t0 + n], in1=f[:, 0:n],
                                                       op=ALU.add),
                      reads=[("fS", fi), ("acc", t0)], writes=[("mT", dc, t0)])

    for dc in range(NDC):
        for nb in range(3):
            wsb, wbv = load_block(C, wsl, "wb", 8, np.arange(dc * 128, (dc + 1) * 128) + 0, cap=2048) if False else \
                load_block(C, wsl, ("wb", nb), 8, np.arange(dc * 128, (dc + 1) * 128), cap=2048)
            wsm, mgv = load_block(C, wsl, "w_in", NDC,
                                  np.arange(MG + nb * 2048 + dc * 128, MG + nb * 2048 + (dc + 1) * 128), cap=2048)
            for (t0, n) in TT:
                merge_tile(dc, nb, wsb, wbv, wsm, mgv, t0, n)

    def out_tile(ws, wv, dd, dco, t0, n):
        b = C.nextbank()
        for c in range(NDC):
            S.add("pe", lambda e, c=c: e.matmul(ps[:, b, 0:n], wv[:, c, dd * 128:(dd + 1) * 128], mergedT[:, c, t0:t0 + n],
                                                start=(c == 0), stop=(c == NDC - 1)),
                  reads=[("wsl", ws), ("mT", c, t0)], writes=[("ps", b)])
        S.add("dve", lambda e: e.tensor_tensor(out=C.hT[:, dco, t0:t0 + n], in0=ps[:, b, 0:n], in1=C.hT[:, dco, t0:t0 + n],
                                               op=ALU.add),
              reads=[("ps", b), ("hT", dco)], writes=[("hT", dco)])

    for dco in range(NDC):
        ws, wv = load_block(C, wsl, "wout", NDC, np.arange(dco * 128, (dco + 1) * 128), cap=2048)
        for (t0, n) in TT:
            out_tile(ws, wv, 0, dco, t0, n)
    S.barrier()


def lay_wgu(wg, wu):
    out = np.empty((NFC, 128, 2, NDC, 128), np.float32)
    out[:, :, 0] = wg.reshape(NDC, 128, NFC, 128).transpose(2, 1, 0, 3)
    out[:, :, 1] = wu.reshape(NDC, 128, NFC, 128).transpose(2, 1, 0, 3)
    return out.reshape(NFC, 128, 2 * NDC * 128)


def lay_wd(wd):
    out = np.zeros((4, 8, 128, GF, 256), np.float32)
    w4 = wd.reshape(NFC, 128, 8, 256)
    for gi, (f0, nf) in enumerate(FGROUPS):
        out[gi, :, :, :nf, :] = w4[f0:f0 + nf].transpose(2, 1, 0, 3)
    return out.reshape(4, 8, 128, GF * 256)


def col_lay(v):
    n = v.shape[0] // 128
    return np.ascontiguousarray(v.reshape(n, 128).T)


def make_wblk(blocks, srcs):
    parts = []
    for (src, RC, cols) in blocks:
        M = srcs[src]
        blk = M[:, cols].reshape(RC, 128, len(cols)).transpose(1, 0, 2).reshape(128, RC * len(cols))
        parts.append(blk)
    return np.ascontiguousarray(np.concatenate(parts, axis=1))


def host_consts():
    j = np.arange(128)
    ch = j // 64
    same = ch[:, None] == ch[None, :]
    tri = (j[:, None] <= j[None, :]) & same
    mid = ch * 64 + 31
    trimid = (j[:, None] <= mid[None, :]) & same
    Mq = tri.astype(np.float32) - trimid.astype(np.float32)
    CI = np.stack([(ch == 0), (ch == 1), (ch == 0) & (j <= 31), (ch == 1) & (j <= 95)], axis=1).astype(np.float32)
    U = ((j[:, None] > j[None, :]) & same).astype(np.float32)
    mask = tri.astype(np.float32)
    cst = np.zeros((128, 408), np.float32)
    cst[:, 0:128] = U
    cst[:, 128:256] = Mq
    cst[:, 256:260] = CI
    Mq16 = np.zeros((128, 20), np.float32)
    Mq16[:16, :16] = tri[:16, :16].astype(np.float32) - 1.0
    Mq16[:16, 16] = 1.0
    Mq16[:16, 18] = 1.0
    cst[:, 260:280] = Mq16
    cst[:, 280:408] = mask
    return cst


def rot_tables(jseg):
    pos = np.concatenate([np.arange(16), 16 + jseg * 1024 + np.arange(1024)]).astype(np.float32)
    half = 128
    inv_freq = (np.float32(10000.0) ** (-np.arange(half, dtype=np.float32) / np.float32(half))).astype(np.float32)
    ang = (pos[:, None] * inv_freq[None, :]).astype(np.float32)
    cos, sin = np.cos(ang).astype(np.float32), np.sin(ang).astype(np.float32)
    rotF = np.concatenate([cos.T, sin.T], axis=1)
    rt = np.zeros((2, 128, 9, 128), np.float32)
    for k, tab in enumerate((cos, sin)):
        rt[k, :16, 0, :] = tab[0:16]
        rt[k, :, 1:, :] = tab[16:].reshape(8, 128, 128).transpose(1, 0, 2)
    rotT = np.concatenate([rt[0].reshape(128, 1152), rt[1].reshape(128, 1152)], axis=1)
    return np.ascontiguousarray(rotF), np.ascontiguousarray(rotT)


_PROGS = {}


def get_prog():
    if "F" not in _PROGS:
        st = [(st_load_h, "hin")]
        for L in range(2):
            st += [(st_set_layer, L), (st_ffn, f"wgu{L}a", f"wd{L}a", P_FFN1), (st_mixer, "state"),
                   (st_mixer, "full"), (st_merge,), (st_ffn, f"wgu{L}b", f"wd{L}b", P_FFN2)]
        st += [(st_final_norm,), (st_store_h, "hfin")]
        _PROGS["F"] = build_program(st)
    return _PROGS["F"]


def kernel(x, meta_tokens, ffn1_norm, ffn1_w_gate, ffn1_w_up, ffn1_w_down, mix_norm, w_in,
           gla_w_gate2, gla_b_gate, gla_norm, ret_norm, hgrn_lb_logits, hgrn_norm, w_branch, w_out,
           ffn2_norm, ffn2_w_gate, ffn2_w_up, ffn2_w_down, final_norm):
    f = lambda a: np.asarray(a, dtype=np.float32)
    x = f(x)
    ncores = 8
    nc, C = get_prog()
    shared = {"cst": host_consts()}
    prm = np.zeros((128, 512), np.float32)
    srcs = {}
    for layer in range(2):
        pb = 256 * layer
        prm[:, pb + P_FFN1:pb + P_FFN1 + 16] = col_lay(f(ffn1_norm)[layer])
        prm[:, pb + P_MIX:pb + P_MIX + 16] = col_lay(f(mix_norm)[layer])
        prm[:, pb + P_FFN2:pb + P_FFN2 + 16] = col_lay(f(ffn2_norm)[layer])
        prm[:, pb + 64:pb + 72] = col_lay(f(gla_norm)[layer])
        prm[:, pb + 72:pb + 80] = col_lay(f(ret_norm)[layer])
        prm[:, pb + 80:pb + 88] = col_lay(f(hgrn_norm)[layer])
        prm[:, pb + P_LB0:pb + P_LB0 + 8] = col_lay(f(hgrn_lb_logits)[0])
        prm[:, pb + P_LB1:pb + P_LB1 + 8] = col_lay(f(hgrn_lb_logits)[1])
        prm[:, pb + 104] = float(layer)
        shared[f"w2b{layer}"] = np.concatenate([f(gla_w_gate2)[layer], f(gla_b_gate)[layer][None, :]], axis=0)
        srcs[("w_in", layer)] = f(w_in)[layer]
        srcs[("wout", layer)] = f(w_out)[layer]
        for nb in range(3):
            srcs[(("wb", nb), layer)] = f(w_branch)[layer, nb]
        shared[f"wgu{layer}a"] = lay_wgu(f(ffn1_w_gate)[layer], f(ffn1_w_up)[layer])
        shared[f"wd{layer}a"] = lay_wd(f(ffn1_w_down)[layer])
        shared[f"wgu{layer}b"] = lay_wgu(f(ffn2_w_gate)[layer], f(ffn2_w_up)[layer])
        shared[f"wd{layer}b"] = lay_wd(f(ffn2_w_down)[layer])
    prm[:, P_FIN:P_FIN + 16] = col_lay(f(final_norm))
    shared["prm"] = prm
    shared["wblk"] = make_wblk(C.blocks, srcs)
    shared["lbl"] = np.ascontiguousarray(np.broadcast_to(f(hgrn_lb_logits).reshape(1, 2048), (128, 2048)))
    rots = [rot_tables(j) for j in range(4)]
    maps = []
    for c in range(ncores):
        b, j = c // 4, c % 4
        h0 = np.zeros((T, D), np.float32)
        if j == 0:
            h0[0:16] = f(meta_tokens)
        h0[16:] = x[b, j * 1024:(j + 1) * 1024]
        cm = np.zeros((128, 8), np.float32)
        for s in range(3):
            cm[:, s] = 1.0 if s == j else 0.0
            cm[:, 4 + s] = 1.0 if s < j else 0.0
        m = dict(shared)
        m.update({"hin": np.ascontiguousarray(h0.T), "cm": cm, "rotF": rots[j][0], "rotT": rots[j][1]})
        maps.append(m)
    res = run_bass_kernel_spmd(nc, maps, core_ids=list(range(ncores))).results
    out = np.empty((2, 4096, D), np.float32)
    for c in range(ncores):
        b, j = c // 4, c % 4
        out[b, j * 1024:(j + 1) * 1024] = res[c]["hfin"][:, 16:].T
    return out
```
